# Optimizing a Trainium2 kernel written in Bass

```python
import math
import jax, jax.numpy as jnp
from jax import lax
import numpy as np

D_MODEL = 1024
BATCH = 4
SEQ = 8192
DEPTH = 2

N_MIXERS = 2
HEAD_DIM = 64
N_MIX_HEADS = 12
W_MIX = N_MIX_HEADS * HEAD_DIM
N_MEM_HEADS = 4
W_MEM = N_MEM_HEADS * HEAD_DIM
N_MEM = 256
GMLP_CHUNK = 128
MOBA_BLOCK = 256
MOBA_TOPK = 3
MOBA_Q_BLOCK = 64
D_FF = int(math.ceil(8 * D_MODEL / 3 / 256) * 256)
ALPHA = (2.0 * DEPTH) ** 0.25
BETA = (8.0 * DEPTH) ** -0.25
LN_EPS = 1e-5

kernel_name = "hybrid_gmlp_moba_memxattn_deepnorm"


def layer_norm(x, g, b):
    xf = x.astype(jnp.float32)
    mu = jnp.mean(xf, axis=-1, keepdims=True)
    var = jnp.mean(jnp.square(xf - mu), axis=-1, keepdims=True)
    y = (xf - mu) * lax.rsqrt(var + LN_EPS)
    return (y * g.astype(jnp.float32) + b.astype(jnp.float32)).astype(x.dtype)


def gmlp_mixer(u, v, ln_g, ln_b, w_s, b_s):
    B, S, _ = u.shape
    u = jax.nn.gelu(u)
    v = layer_norm(jax.nn.gelu(v), ln_g, ln_b)
    vc = v.reshape(B, S // GMLP_CHUNK, GMLP_CHUNK, N_MIX_HEADS, HEAD_DIM)
    causal = jnp.tril(jnp.ones((GMLP_CHUNK, GMLP_CHUNK), w_s.dtype))
    w = w_s * causal[None]
    sv = jnp.einsum('gts,bcsgd->bctgd', w, vc) + b_s.T[None, None, :, :, None]
    return u * sv.reshape(B, S, W_MIX)


def moba_attention(q, k, v):
    B, S, H, dh = q.shape
    n_blk = -(-S // MOBA_BLOCK)
    Sp = n_blk * MOBA_BLOCK
    pad = ((0, 0), (0, Sp - S), (0, 0), (0, 0))
    qh = jnp.pad(q, pad).transpose(0, 2, 1, 3)
    kb = jnp.pad(k, pad).transpose(0, 2, 1, 3).reshape(B, H, n_blk, MOBA_BLOCK, dh)
    vb = jnp.pad(v, pad).transpose(0, 2, 1, 3).reshape(B, H, n_blk, MOBA_BLOCK, dh)
    k_mean = jnp.mean(kb.astype(jnp.float32), axis=3)
    topk = min(MOBA_TOPK, n_blk)
    scale = dh ** -0.5
    b_idx = jnp.arange(B)[:, None, None, None]
    h_idx = jnp.arange(H)[None, :, None, None]

    def one_query_block(c):
        start = c * MOBA_Q_BLOCK
        qc = lax.dynamic_slice_in_dim(qh, start, MOBA_Q_BLOCK, axis=2)
        j = start // MOBA_BLOCK
        blk_s = jnp.einsum('bhqd,bhnd->bhqn', qc.astype(jnp.float32), k_mean)
        blk_s = jnp.where(jnp.arange(n_blk) < j, blk_s, -jnp.inf)
        _, sel = lax.top_k(blk_s, topk)
        valid = jnp.arange(topk) < j
        k_sel = kb[b_idx, h_idx, sel]
        v_sel = vb[b_idx, h_idx, sel]
        s_sel = jnp.einsum('bhqd,bhqnkd->bhqnk', qc, k_sel).astype(jnp.float32) * scale
        s_sel = jnp.where(valid[:, None], s_sel, -jnp.inf)
        k_own = lax.dynamic_index_in_dim(kb, j, axis=2, keepdims=False)
        v_own = lax.dynamic_index_in_dim(vb, j, axis=2, keepdims=False)
        s_own = jnp.einsum('bhqd,bhkd->bhqk', qc, k_own).astype(jnp.float32) * scale
        q_pos = start + jnp.arange(MOBA_Q_BLOCK)
        k_pos = j * MOBA_BLOCK + jnp.arange(MOBA_BLOCK)
        s_own = jnp.where(k_pos[None, :] <= q_pos[:, None], s_own, -jnp.inf)
        s_all = jnp.concatenate(
            [s_sel.reshape(B, H, MOBA_Q_BLOCK, topk * MOBA_BLOCK), s_own], axis=-1)
        p = jax.nn.softmax(s_all, axis=-1)
        p_sel = p[..., :topk * MOBA_BLOCK].reshape(B, H, MOBA_Q_BLOCK, topk, MOBA_BLOCK)
        p_own = p[..., topk * MOBA_BLOCK:]
        out = (jnp.einsum('bhqnk,bhqnkd->bhqd', p_sel.astype(v.dtype), v_sel)
               + jnp.einsum('bhqk,bhkd->bhqd', p_own.astype(v.dtype), v_own))
        return out

    outs = lax.map(one_query_block, jnp.arange(Sp // MOBA_Q_BLOCK))
    out = outs.transpose(1, 2, 0, 3, 4).reshape(B, H, Sp, dh)
    return out.transpose(0, 2, 1, 3)[:, :S]


def memory_cross_attention(q_mem, mem, w_kv):
    B, S, _ = q_mem.shape
    q = q_mem.reshape(B, S, N_MEM_HEADS, HEAD_DIM)
    kv = mem @ w_kv
    k, v = jnp.split(kv, 2, axis=-1)
    k = k.reshape(B, -1, N_MEM_HEADS, HEAD_DIM)
    v = v.reshape(B, -1, N_MEM_HEADS, HEAD_DIM)
    s = jnp.einsum('bshd,bmhd->bhsm', q, k).astype(jnp.float32) * HEAD_DIM ** -0.5
    p = jax.nn.softmax(s, axis=-1).astype(v.dtype)
    return jnp.einsum('bhsm,bmhd->bshd', p, v).reshape(B, S, W_MEM)


def swiglu(x, w_in, w_out):
    g, u = jnp.split(x @ w_in, 2, axis=-1)
    return (jax.nn.silu(g) * u) @ w_out


def setup_inputs(seed: int = 0) -> dict:
    key = jax.random.key(seed)
    ks = jax.random.split(key, 16)
    n_a = (DEPTH + 1) // 2
    n_b = DEPTH // 2
    D = D_MODEL
    nrm = jax.random.normal
    f32 = jnp.float32
    return {
        "x": nrm(ks[0], (BATCH, SEQ, D), f32),
        "mem": nrm(ks[1], (BATCH, N_MEM, D), f32),
        "a_w_in": nrm(ks[2], (n_a, D, 2 * W_MIX + W_MEM), f32) * D ** -0.5,
        "a_ln_v_g": 1.0 + 0.02 * nrm(ks[3], (n_a, W_MIX), f32),
        "a_ln_v_b": 0.02 * nrm(ks[4], (n_a, W_MIX), f32),
        "a_w_s": nrm(ks[5], (n_a, N_MIX_HEADS, GMLP_CHUNK, GMLP_CHUNK), f32) * GMLP_CHUNK ** -0.5,
        "a_b_s": 1.0 + 0.1 * nrm(ks[6], (n_a, N_MIX_HEADS, GMLP_CHUNK), f32),
        "b_w_in": nrm(ks[7], (n_b, D, 3 * W_MIX + W_MEM), f32) * D ** -0.5,
        "w_mem_kv": nrm(ks[8], (DEPTH, D, 2 * W_MEM), f32) * D ** -0.5,
        "w_mix_out": nrm(ks[9], (DEPTH, W_MIX + W_MEM, D), f32) * (W_MIX + W_MEM) ** -0.5 * BETA,
        "ln_mix_g": 1.0 + 0.02 * nrm(ks[10], (DEPTH, D), f32),
        "ln_mix_b": 0.02 * nrm(ks[11], (DEPTH, D), f32),
        "w_ffn_in": nrm(ks[12], (DEPTH, D, 2 * D_FF), f32) * D ** -0.5,
        "w_ffn_out": nrm(ks[13], (DEPTH, D_FF, D), f32) * D_FF ** -0.5 * BETA,
        "ln_ffn_g": 1.0 + 0.02 * nrm(ks[14], (DEPTH, D), f32),
        "ln_ffn_b": 0.02 * nrm(ks[15], (DEPTH, D), f32),
    }


def reference(x, mem, a_w_in, a_ln_v_g, a_ln_v_b, a_w_s, a_b_s, b_w_in, w_mem_kv,
              w_mix_out, ln_mix_g, ln_mix_b, w_ffn_in, w_ffn_out, ln_ffn_g, ln_ffn_b):
    B, S, _ = x.shape
    for i in range(DEPTH):
        li = i // N_MIXERS
        if i % N_MIXERS == 0:
            h = x @ a_w_in[li]
            u, v, q_mem = jnp.split(h, [W_MIX, 2 * W_MIX], axis=-1)
            mix = gmlp_mixer(u, v, a_ln_v_g[li], a_ln_v_b[li], a_w_s[li], a_b_s[li])
        else:
            h = x @ b_w_in[li]
            q, k, v, q_mem = jnp.split(h, [W_MIX, 2 * W_MIX, 3 * W_MIX], axis=-1)
            shp = (B, S, N_MIX_HEADS, HEAD_DIM)
            mix = moba_attention(q.reshape(shp), k.reshape(shp), v.reshape(shp)).reshape(B, S, W_MIX)
        mem_out = memory_cross_attention(q_mem, mem, w_mem_kv[i])
        sub = jnp.concatenate([mix, mem_out], axis=-1) @ w_mix_out[i]
        x = layer_norm(ALPHA * x + sub, ln_mix_g[i], ln_mix_b[i])
        x = layer_norm(ALPHA * x + swiglu(x, w_ffn_in[i], w_ffn_out[i]), ln_ffn_g[i], ln_ffn_b[i])
    return x
```

```python
import os
import numpy as np
import ml_dtypes
import concourse.bass as bass
import concourse.mybir as mybir
from concourse.bass_utils import run_bass_kernel_spmd

F32 = mybir.dt.float32
BF16 = mybir.dt.bfloat16
AF = mybir.ActivationFunctionType
ALU = mybir.AluOpType
AX = mybir.AxisListType

ENGS = ("pe", "act", "dve", "pool", "sp")

D = 1024
SEQ = 8192
NH = 12
DH = 64
WMIX = 768
NFF = 22
DFF = 2816
ALPHA = 4.0 ** 0.25
LN_EPS = 1e-5
NEGV = 30000.0
NG1 = 16
NG2 = 8


class Res:
    __slots__ = ("name", "last_w", "readers")

    def __init__(self, name):
        self.name = name
        self.last_w = None
        self.readers = []


class Op:
    __slots__ = ("eng", "fn", "deps", "is_dma", "dsem", "token", "signal", "idx", "inc")

    def __init__(self, eng, fn, is_dma=False, dsem=None):
        self.eng = eng
        self.fn = fn
        self.deps = []
        self.is_dma = is_dma
        self.dsem = dsem
        self.token = None
        self.signal = is_dma
        self.idx = None
        self.inc = 16


class Prog:
    def __init__(self):
        self.ops = []
        self.dma_counts = {}
        self.last_dma = {}
        self.last_eng = {}
        self.barrier_op = None

    def op(self, eng, fn, reads=(), writes=(), is_dma=False, dsem=None, inc=16, extra_deps=()):
        o = Op(eng, fn, is_dma, dsem)
        o.inc = inc
        o.idx = len(self.ops)
        deps = {}
        for r in reads:
            w = r.last_w
            if w is not None:
                deps[w.idx] = w
        for r in writes:
            w = r.last_w
            if w is not None and (w.is_dma or is_dma or w.eng != eng):
                if not (w.is_dma and is_dma and w.dsem == dsem):
                    deps[w.idx] = w
            for rd in r.readers:
                if rd.is_dma or is_dma or rd.eng != eng:
                    deps[rd.idx] = rd
        for p in extra_deps:
            deps[p.idx] = p
        if self.barrier_op is not None:
            deps[self.barrier_op.idx] = self.barrier_op
        for r in reads:
            r.readers.append(o)
        for r in writes:
            r.last_w = o
            r.readers = []
        deps.pop(o.idx, None)
        o.deps = list(deps.values())
        for p in o.deps:
            p.signal = True
        if is_dma:
            c = self.dma_counts.get(dsem, 0) + inc
            self.dma_counts[dsem] = c
            o.token = (("dma", dsem), c)
            self.last_dma[dsem] = o
        else:
            self.last_eng[eng] = o
        self.ops.append(o)
        return o

    def barrier(self, fn):
        deps = list(self.last_dma.values()) + list(self.last_eng.values())
        self.barrier_op = None
        b = self.op("sp", fn, is_dma=True, dsem="barrier%d" % len(self.ops), extra_deps=deps)
        self.barrier_op = b
        return b

    def finalize_and_emit(self, nc, final_wait_eng="sp"):
        cnt = {e: 0 for e in ENGS}
        for o in self.ops:
            if not o.is_dma and o.signal:
                cnt[o.eng] += 1
                o.token = (("eng", o.eng), cnt[o.eng])
        sem_keys = [("eng", e) for e in ENGS] + [("dma", k) for k in self.dma_counts]
        sems = {}
        for k in sem_keys:
            sems[k] = nc.alloc_semaphore(name="s_%s_%s" % (k[0], str(k[1])))
        seen = {e: {} for e in ENGS}
        per_eng = {e: [] for e in ENGS}
        for o in self.ops:
            mw = {}
            for p in o.deps:
                key, val = p.token
                if seen[o.eng].get(key, 0) >= val:
                    continue
                seen[o.eng][key] = val
                mw[key] = max(mw.get(key, 0), val)
            per_eng[o.eng].append((o, list(mw.items())))
        finals = [(("dma", k), c) for k, c in self.dma_counts.items()]
        n_inst = {e: 0 for e in ENGS}
        with nc.Block() as block:
            def run(engname, engobj):
                for o, waits in per_eng[engname]:
                    for k, v in waits:
                        engobj.wait_ge(sems[k], v)
                        n_inst[engname] += 1
                    ins = o.fn(engobj)
                    n_inst[engname] += 1
                    if o.is_dma:
                        ins.then_inc(sems[o.token[0]], o.inc)
                    elif o.signal:
                        ins.then_inc(sems[o.token[0]], 1)
                if engname == final_wait_eng:
                    for k, v in finals:
                        if seen[engname].get(k, 0) < v:
                            engobj.wait_ge(sems[k], v)

            @block.tensor
            def _(e):
                run("pe", e)

            @block.scalar
            def _(e):
                run("act", e)

            @block.vector
            def _(e):
                run("dve", e)

            @block.gpsimd
            def _(e):
                run("pool", e)

            @block.sync
            def _(e):
                run("sp", e)
        return n_inst


class Buf:
    __slots__ = ("ap", "r")

    def __init__(self, ap, name):
        self.ap = ap
        self.r = Res(name)


def build_program(debug=False, ng1=NG1, ng2=NG2, cut=99):
    nc = bass.Bass("TRN2", target_bir_lowering=False)
    P = Prog()

    def din(name, shape, dt=F32):
        return nc.dram_tensor(name, list(shape), dt, kind="ExternalInput").ap()

    x_d = din("x", [SEQ, D])
    mem_d = din("mem", [256, D])
    a_w_in = din("a_w_in", [D, 1792])
    a_w_s = din("a_w_s", [12, 128, 128])
    b_w_in = din("b_w_in", [D, 2560])
    w_mem_kv = din("w_mem_kv", [2, D, 512])
    w_mix_out = din("w_mix_out", [2, D, D])
    w_ffn_in = din("w_ffn_in", [2, D, 2 * DFF])
    w_ffn_out = din("w_ffn_out", [2, DFF, D])
    lnrep_d = din("lnrep", [2, 128, 4, D])
    lnv_d = din("lnv", [128, 2, WMIX])
    lncol_d = din("lncol", [2, 128, 4, 8])
    bsT_d = din("bsT", [128, 6, 128])
    negh_d = din("negh", [128, 1])
    ident_d = din("ident", [128, 128], BF16)
    tril_d = din("tril", [128, 128])
    trim_d = din("trimask", [128, 2, 256], BF16)
    pastneg_d = din("pastneg", [128, 16, 32])
    ohA_d = din("ohA", [32, 4096], BF16)
    ohB_d = din("ohB", [32, 4096], BF16)
    out_d = nc.dram_tensor("out", [4096, D], F32, kind="ExternalOutput").ap()
    def dscr(name, shape, dt):
        return nc.dram_tensor(name, list(shape), dt, kind="ExternalOutput").ap()

    x1_d = dscr("x1dbg", [4096, D], F32)
    qT_d = dscr("qTs", [768, 4096], BF16)
    qmT_d = dscr("qmTs", [256, 4096], BF16)
    kT_d = dscr("kTs", [768, SEQ], BF16)
    v_d = dscr("vs", [12, 128, 64, 128], BF16)
    km_d = dscr("kms", [768, 32], BF16)
    bar_d = nc.dram_tensor("bars", [128, 1], F32).ap()
    R_x1d, R_qTd, R_qmTd, R_kTd, R_vd, R_kmd = (Res(n) for n in ("x1d", "qTd", "qmTd", "kTd", "vd", "kmd"))

    def sb(name, shape, dt):
        return Buf(nc.alloc_sbuf_tensor("sb_" + name, list(shape), dt)[:], name)

    def ps(name, shape, dt=F32):
        return Buf(nc.alloc_psum_tensor(name, list(shape), dt)[:], name)

    xb = sb("xb", [128, 4, D], F32)
    xbR = [Res("xb%d" % i) for i in range(4)]
    xT = sb("xT", [128, 8, 512], BF16)
    xT2 = sb("xT2", [128, 8, 512], BF16)
    scr_t = nc.alloc_sbuf_tensor("sb_scr", [128, NFF, 512], BF16)
    scrR = [Res("scr%d" % i) for i in range(NFF)]
    wout_t = nc.alloc_sbuf_tensor("sb_wout", [128, NFF, D], BF16)
    woutR = [Res("wout%d" % i) for i in range(11)]
    wsl = [sb("wsl%d" % i, [128, 8, 512], BF16) for i in range(3)]
    lnp = sb("lnp", [128, 4, D], F32)
    lncol = sb("lncol", [128, 4, 8], F32)
    ident = sb("ident", [128, 128], BF16)
    kmemT = sb("kmemT", [128, 4, 256], BF16)
    vmem = sb("vmem", [128, 16, 128], BF16)
    tmpbf = [sb("tmpbf%d" % i, [128, D], BF16) for i in range(4)]
    tbq = tmpbf
    tA = [sb("tA%d" % i, [128, D], F32) for i in range(2)]
    PT2 = [sb("PT2_%d" % i, [128, 1024], BF16) for i in range(2)]
    PT = PT2
    negh = sb("negh", [128, 1], F32)
    epsb = sb("epsb", [128, 1], F32)
    bnst = [sb("bnst%d" % i, [128, 2, 6], F32) for i in range(4)]
    mv = [sb("mv%d" % i, [128, 2], F32) for i in range(4)]
    rstd = [sb("rstd%d" % i, [128, 1], F32) for i in range(4)]
    nbias = [sb("nbias%d" % i, [128, 1], F32) for i in range(4)]
    UNI = 40 * 1024
    uni_t = nc.alloc_sbuf_tensor("sb_uni", [128, UNI // 2], BF16)

    class Carver:
        def __init__(self):
            self.off = 0

        def take(self, name, shape, dt, parts=128):
            n = int(np.prod(shape[1:]))
            nb = n * (4 if dt == F32 else 2)
            nb_al = (nb + 63) // 64 * 64
            a = uni_t[0:parts, self.off // 2:(self.off + nb) // 2]
            if dt == F32:
                a = a.bitcast(F32)
            if len(shape) == 3:
                a = a.rearrange("p (a b) -> p a b", b=shape[2])
            elif len(shape) == 4:
                a = a.rearrange("p (a b c) -> p a b c", b=shape[2], c=shape[3])
            self.off += nb_al
            assert self.off <= UNI, (name, self.off)
            return Buf(a, name)

    c1 = Carver()
    vst = [c1.take("vst%d" % i, [128, 12, 128], BF16) for i in range(2)]
    stg = [c1.take("stg%d" % i, [128, 512], BF16) for i in range(3)]
    lnv = c1.take("lnv", [128, 2, WMIX], F32)
    bsT = c1.take("bsT", [128, 6, 128], F32)
    WsT = c1.take("WsT", [128, 12, 128], BF16)
    kmacc = c1.take("kmacc", [128, 6, 32], F32)
    kmbf = c1.take("kmbf", [128, 6, 32], BF16)
    c2 = Carver()
    KA = c2.take("KA", [96, 4096], BF16, parts=96)
    KB = c2.take("KB", [96, 4096], BF16, parts=96)
    VA = c2.take("VA", [128, 32, 128], BF16)
    VB = c2.take("VB", [128, 32, 128], BF16)
    Qaug = [c2.take("Qaug%d" % i, [96, 512], BF16, parts=96) for i in range(2)]
    pastneg = c2.take("pastneg", [128, 16, 32], F32)
    trim = c2.take("trim", [128, 2, 256], BF16)
    kmT = c2.take("kmT", [64, 12, 32], BF16, parts=64)
    scm = c2.take("scm", [128, 4, 32], F32)
    m8 = c2.take("m8", [128, 4, 8], F32)
    thr = c2.take("thr", [128, 4], F32)
    biasin = [c2.take("biasin%d" % i, [128, 4, 96], BF16) for i in range(2)]
    NSEG = 4
    KR = [[Res("K%d_%d" % (pt_, sg)) for sg in range(NSEG)] for pt_ in range(2)]
    VR = [[Res("V%d_%d" % (pt_, sg)) for sg in range(NSEG)] for pt_ in range(2)]

    psSS_t = nc.alloc_psum_tensor("psSS", [128, 1024], F32)
    psS = [Buf(psSS_t[:, 512 * i:512 * (i + 1)], "psS%d" % i) for i in range(2)]
    psO = [ps("psO%d" % i, [128, 512]) for i in range(2)]
    psW = ps("psW", [128, 1024])
    psM_t = nc.alloc_psum_tensor("psM", [128, 1024], BF16)
    psSC = Buf(psM_t[:, 0:256].bitcast(F32).rearrange("p (a b) -> p a b", b=32), "psSC")
    psBT = Buf(psM_t[0:96, 512:1024], "psBT")
    psBT.r = psSC.r
    psT = ps("psT", [128, 1024], BF16)

    cnt = {"s": 0, "o": 0, "w": 0, "pt": 0, "stg": 0, "tb": 0, "ta": 0, "st": 0}

    def rr(key, lst):
        i = cnt[key]
        cnt[key] = i + 1
        return lst[i % len(lst)]

    def dma(q, out, in_, reads, writes, dsem):
        return P.op(q, lambda e: e.dma_start(out=out, in_=in_), reads=reads, writes=writes, is_dma=True, dsem=dsem)

    def mm(out, lhsT, rhs, start, stop, reads, writes, tile_position=None):
        if tile_position is None:
            return P.op("pe", lambda e: e.matmul(out, lhsT=lhsT, rhs=rhs, start=start, stop=stop), reads=reads, writes=writes)
        return P.op("pe", lambda e: e.matmul(out, lhsT=lhsT, rhs=rhs, start=start, stop=stop, tile_position=tile_position),
                    reads=reads, writes=writes)

    def tr(out, in_, reads, writes):
        n = in_.shape[0]
        return P.op("pe", lambda e: e.transpose(out=out, in_=in_, identity=ident.ap[0:n, 0:n]), reads=list(reads) + [ident.r], writes=writes)

    def act(out, in_, func, reads, writes, scale=1.0, bias=0.0):
        return P.op("act", lambda e: e.activation(out=out, in_=in_, func=func, bias=bias, scale=scale), reads=reads, writes=writes)

    def dve(fn, reads, writes):
        return P.op("dve", fn, reads=reads, writes=writes)

    def tt(out, in0, in1, op, reads, writes, eng="dve"):
        return P.op(eng, lambda e: e.tensor_tensor(out=out, in0=in0, in1=in1, op=op), reads=reads, writes=writes)

    def ts(out, in0, s1, s2, op0, op1, reads, writes):
        if s2 is None:
            return P.op("dve", lambda e: e.tensor_scalar(out=out, in0=in0, scalar1=s1, scalar2=None, op0=op0), reads=reads, writes=writes)
        return P.op("dve", lambda e: e.tensor_scalar(out=out, in0=in0, scalar1=s1, scalar2=s2, op0=op0, op1=op1), reads=reads, writes=writes)

    wcnt = [0]

    def wload(src3, width):
        s = wsl[wcnt[0] % 3]
        wcnt[0] += 1
        dma("pool", s.ap[:, :, 0:width], src3, [], [s.r], "wsl_" + s.r.name)
        return s

    def wpiece(w2d, c0, width):
        return w2d.rearrange("(kc p) c -> p kc c", p=128)[:, :, c0:c0 + width]

    def scr(i):
        return scr_t[:, i, :]

    dma("sp", ident.ap, ident_d, [], [ident.r], "ident")
    dma("sp", negh.ap, negh_d, [], [negh.r], "negh")
    P.op("dve", lambda e: e.memset(epsb.ap, LN_EPS), writes=[epsb.r])
    dma("sp", lnv.ap, lnv_d, [], [lnv.r], "lnv")
    dma("sp", bsT.ap, bsT_d, [], [bsT.r], "bsT")
    dma("sp", lnp.ap, lnrep_d[0], [], [lnp.r], "lnp")
    dma("sp", lncol.ap, lncol_d[0], [], [lncol.r], "lncol")
    wsf = tA[0]
    trl = tA[1]
    wsm = tmpbf[0]
    dma("sp", trl.ap[:, 0:128], tril_d, [], [trl.r], "tA1")
    for half in range(2):
        dma("sp", wsf.ap[:, 0:768].rearrange("p (g s) -> p g s", s=128), a_w_s[6 * half:6 * half + 6].rearrange("g t s -> t g s"),
            [], [wsf.r], "tA0")
        for gl in range(6):
            tt(wsm.ap[:, gl * 128:(gl + 1) * 128], wsf.ap[:, gl * 128:(gl + 1) * 128], trl.ap[:, 0:128], ALU.mult,
               [wsf.r, trl.r], [wsm.r])
        for gl in range(6):
            tr(psT.ap[:, gl * 128:(gl + 1) * 128], wsm.ap[:, gl * 128:(gl + 1) * 128], [wsm.r], [psT.r])
        dve(lambda e, half=half: e.tensor_copy(out=WsT.ap[:, half * 6:(half + 1) * 6, :],
                                               in_=psT.ap[:, 0:768].rearrange("p (g t) -> p g t", t=128)),
            [psT.r], [WsT.r])

    memT = xT2
    for mc in range(2):
        dma("sp", tA[mc].ap, mem_d[mc * 128:(mc + 1) * 128, :], [], [tA[mc].r], "tA%d" % mc)
        act(tmpbf[mc].ap, tA[mc].ap, AF.Copy, [tA[mc].r], [tmpbf[mc].r])
        for kc in range(8):
            tr(psT.ap[:, kc * 128:(kc + 1) * 128], tmpbf[mc].ap[:, kc * 128:(kc + 1) * 128], [tmpbf[mc].r], [psT.r])
        dve(lambda e, mc=mc: e.tensor_copy(out=memT.ap[:, :, mc * 128:(mc + 1) * 128],
                                           in_=psT.ap.rearrange("p (k t) -> p k t", t=128)), [psT.r], [memT.r])
    P.op("dve", lambda e: e.memset(vmem.ap, 1.0), writes=[vmem.r])
    for L in range(2):
        w = wload(wpiece(w_mem_kv[L], 0, 512), 512)
        for pr in range(2):
            o = rr("s", psS)
            for kc in range(8):
                mm(o.ap[:, 0:256], w.ap[:, kc, pr * 128:(pr + 1) * 128], memT.ap[:, kc, 0:256], kc == 0, kc == 7, [w.r, memT.r], [o.r])
            act(kmemT.ap[:, L * 2 + pr, :], o.ap[:, 0:256], AF.Copy, [o.r], [kmemT.r])
        for mc in range(2):
            o = rr("s", psS)
            for kc in range(8):
                mm(o.ap[:, 0:256], memT.ap[:, kc, mc * 128:(mc + 1) * 128], w.ap[:, kc, 256:512], kc == 0, kc == 7, [w.r, memT.r], [o.r])
            ov = o.ap[:, 0:256].rearrange("p (pr hh d) -> p pr hh d", hh=2, d=64)
            vv = vmem.ap[:, (L * 2 + mc) * 4:(L * 2 + mc) * 4 + 4, :].rearrange("p (pr hh) c -> p pr hh c", hh=2)
            act(vv[:, :, 0, 0:64], ov[:, :, 0, :], AF.Copy, [o.r], [vmem.r])
            act(vv[:, :, 1, 64:128], ov[:, :, 1, :], AF.Copy, [o.r], [vmem.r])

    def transposes_to(dst, s, srcbuf, vi=None):
        for kc in range(8):
            tr(psT.ap[:, kc * 128:(kc + 1) * 128], srcbuf.ap[:, kc * 128:(kc + 1) * 128], [srcbuf.r], [psT.r])
        if vi is None:
            act(dst.ap[:, :, s * 128:(s + 1) * 128], psT.ap.rearrange("p (k t) -> p k t", t=128), AF.Copy, [psT.r], [dst.r])
        else:
            for kc in range(8):
                P.op("act", lambda e, kc=kc: e.activation(out=dst.ap[:, kc, s * 128:(s + 1) * 128], in_=psT.ap[:, kc * 128:(kc + 1) * 128],
                                                         func=AF.Identity, bias=lncol.ap[:, vi + 1, kc:kc + 1], scale=lncol.ap[:, vi, kc:kc + 1]),
                     reads=[psT.r, lncol.r], writes=[dst.r])

    def ln_stats(buf_ap, width, bufR, k):
        nchunk = 2
        cw = width // nchunk
        for c in range(nchunk):
            dve(lambda e, c=c: e.bn_stats(out=bnst[k].ap[:, c, :], in_=buf_ap[:, c * cw:(c + 1) * cw]), [bufR], [bnst[k].r])
        dve(lambda e: e.bn_aggr(out=mv[k].ap, in_=bnst[k].ap.rearrange("p a b -> p (a b)")), [bnst[k].r], [mv[k].r])
        act(nbias[k].ap, mv[k].ap[:, 1:2], AF.Sqrt, [mv[k].r, epsb.r], [nbias[k].r], bias=epsb.ap)
        dve(lambda e: e.reciprocal(out=rstd[k].ap, in_=nbias[k].ap), [nbias[k].r], [rstd[k].r])
        ts(nbias[k].ap, mv[k].ap[:, 0:1], -1.0, rstd[k].ap, ALU.mult, ALU.mult, [mv[k].r, rstd[k].r], [nbias[k].r])

    def layer_norm_v(buf_ap, width, bufR, gam, bet, gbR, outbf_ap, outbfRs, k):
        ln_stats(buf_ap, width, bufR, k)
        P.op("act", lambda e: e.activation(out=buf_ap, in_=buf_ap, func=AF.Identity, bias=nbias[k].ap, scale=rstd[k].ap),
             reads=[bufR, nbias[k].r, rstd[k].r], writes=[bufR])
        tt(buf_ap, buf_ap, gam, ALU.mult, [bufR, gbR], [bufR])
        tt(outbf_ap, buf_ap, bet, ALU.add, [bufR, gbR], list(outbfRs))

    def layer_norm_res(buf_ap, bufR, gi, tb, k):
        ln_stats(buf_ap, D, bufR, k)
        if tb is not None:
            P.op("act", lambda e: e.activation(out=tb.ap, in_=buf_ap, func=AF.Identity, bias=nbias[k].ap, scale=rstd[k].ap),
                 reads=[bufR, nbias[k].r, rstd[k].r], writes=[tb.r])
        P.op("pool", lambda e: e.tensor_scalar(out=buf_ap, in0=buf_ap, scalar1=mv[k].ap[:, 0:1], scalar2=rstd[k].ap,
                                               op0=ALU.subtract, op1=ALU.mult),
             reads=[bufR, mv[k].r, rstd[k].r], writes=[bufR])
        tt(buf_ap, buf_ap, lnp.ap[:, gi, :], ALU.mult, [bufR, lnp.r], [bufR], eng="pool")
        tt(buf_ap, buf_ap, lnp.ap[:, gi + 1, :], ALU.add, [bufR, lnp.r], [bufR], eng="pool")

    def mem_attention(L, qmR):
        for h in range(4):
            pr, hh = h // 2, h % 2
            pb = 64 * hh
            o = rr("o", psO)
            for mc in range(2):
                s_ = rr("s", psS)
                mm(s_.ap, kmemT.ap[pb:pb + 64, L * 2 + pr, mc * 128:(mc + 1) * 128], scr_t[pb:pb + 64, 18 + pr, :], True, True,
                   [kmemT.r, scrR[18 + pr]], [s_.r])
                pt = rr("pt", PT)
                act(pt.ap[:, 0:512], s_.ap, AF.Exp, [s_.r], [pt.r])
                mm(o.ap, vmem.ap[:, (L * 2 + mc) * 4 + h, :], pt.ap[:, 0:512], mc == 0, mc == 1, [vmem.r, pt.r], [o.r])
            normalise_into(o, hh, 20 + pr)

    def normalise_into(o, hh, scr_idx):
        ob, db = (0, 64) if hh == 0 else (64, 0)
        rec = rr("ta", tA)
        dve(lambda e: e.reciprocal(out=rec.ap[db:db + 64, 0:512], in_=o.ap[db:db + 64, :]), [o.r], [rec.r])
        tt(scr_t[ob:ob + 64, scr_idx, :], o.ap[ob:ob + 64, :], rec.ap[db:db + 64, 0:512], ALU.mult, [o.r, rec.r], [scrR[scr_idx]])

    def out_proj_ln_ffn(L, last, G):
        wo = [wload(wpiece(w_mix_out[L], hf * 512, 512), 512) for hf in range(2)]
        for j in range(11):
            dma("pool", wout_t[:, 2 * j:2 * j + 2, :], w_ffn_out[L, 256 * j:256 * (j + 1), :].rearrange("(f p) c -> p f c", p=128),
                [], [woutR[j]], "wout%d" % j)
        for s in range(4):
            for hf in range(2):
                y = rr("o", psO)
                for c in range(8):
                    si = 6 + c if c < 6 else 20 + (c - 6)
                    mm(y.ap, scr_t[:, si, s * 128:(s + 1) * 128], wo[hf].ap[:, c, :], c == 0, c == 7, [scrR[si], wo[hf].r], [y.r])
                xs = xb.ap[:, s, hf * 512:(hf + 1) * 512]
                dve(lambda e, xs=xs, y=y: e.scalar_tensor_tensor(out=xs, in0=xs, scalar=ALPHA, in1=y.ap, op0=ALU.mult, op1=ALU.add),
                    [xbR[s], y.r], [xbR[s]])
            layer_norm_res(xb.ap[:, s, :], xbR[s], 0, tbq[s], s)
        for s in range(4):
            transposes_to(xT2, s, tbq[s], vi=0)
        for j in range(11):
            w = wsl[wcnt[0] % 3]
            wcnt[0] += 1
            for two in range(2):
                dma("pool", w.ap[:, :, 256 * two:256 * (two + 1)], wpiece(w_ffn_in[L], two * DFF + 256 * j, 256), [], [w.r], "wsl_" + w.r.name)
            for fl in range(2):
                f = 2 * j + fl
                pg = rr("s", psS)
                pu = rr("o", psO)
                for kc in range(8):
                    mm(pg.ap, w.ap[:, kc, fl * 128:(fl + 1) * 128], xT2.ap[:, kc, :], kc == 0, kc == 7, [w.r, xT2.r], [pg.r])
                for kc in range(8):
                    mm(pu.ap, w.ap[:, kc, 256 + fl * 128:256 + (fl + 1) * 128], xT2.ap[:, kc, :], kc == 0, kc == 7, [w.r, xT2.r], [pu.r])
                sg = rr("ta", tA)
                act(sg.ap[:, 0:512], pg.ap, AF.Silu, [pg.r], [sg.r])
                tt(scr(f), sg.ap[:, 0:512], pu.ap, ALU.mult, [sg.r, pu.r], [scrR[f]])
        for s in range(4):
            for hf in range(2):
                y = rr("o", psO)
                for f in range(NFF):
                    mm(y.ap, scr_t[:, f, s * 128:(s + 1) * 128], wout_t[:, f, hf * 512:(hf + 1) * 512], f == 0, f == NFF - 1,
                       [scrR[f], woutR[f // 2]], [y.r])
                xs = xb.ap[:, s, hf * 512:(hf + 1) * 512]
                dve(lambda e, xs=xs, y=y: e.scalar_tensor_tensor(out=xs, in0=xs, scalar=ALPHA, in1=y.ap, op0=ALU.mult, op1=ALU.add),
                    [xbR[s], y.r], [xbR[s]])
            layer_norm_res(xb.ap[:, s, :], xbR[s], 2, None if last else tbq[s], s)
        if not last:
            for s in range(4):
                transposes_to(xT, s, tbq[s], vi=2)

    P.op("dve", lambda e: e.memset(kmacc.ap, 0.0), writes=[kmacc.r])
    for i in range(2):
        P.op("dve", lambda e, i=i: e.memset(vst[i].ap, 1.0), writes=[vst[i].r])
    for G in range(ng1):
        own = G < NG2
        t0 = G * 512
        dma("sp", xb.ap, x_d[t0:t0 + 512, :].rearrange("(s p) d -> p s d", p=128), [], xbR, "xb")
        for s in range(4):
            tb = rr("tb", tmpbf)
            act(tb.ap, xb.ap[:, s, :], AF.Copy, [xbR[s]], [tb.r])
            transposes_to(xT, s, tb)
        if cut <= 1:
            continue
        for (c0, wd, base) in ((0, 512, 0), (512, 256, 4)):
            w = wload(wpiece(a_w_in, c0, wd), wd)
            for jj in range(wd // 128):
                o = rr("s", psS)
                for kc in range(8):
                    mm(o.ap, w.ap[:, kc, jj * 128:(jj + 1) * 128], xT.ap[:, kc, :], kc == 0, kc == 7, [w.r, xT.r], [o.r])
                act(scr(base + jj), o.ap, AF.Gelu_apprx_tanh, [o.r], [scrR[base + jj]])
        w = wload(wpiece(a_w_in, 1536, 256), 256)
        for jj in range(2):
            o = rr("s", psS)
            for kc in range(8):
                mm(o.ap, w.ap[:, kc, jj * 128:(jj + 1) * 128], xT.ap[:, kc, :], kc == 0, kc == 7, [w.r, xT.r], [o.r])
            act(scr(18 + jj), o.ap, AF.Copy, [o.r], [scrR[18 + jj]], scale=0.125)
        if cut <= 2:
            continue
        wv0 = wload(wpiece(a_w_in, 768, 512), 512)
        wv1 = wload(wpiece(a_w_in, 1280, 256), 256)
        vR = scrR[12:18]
        v_all = scr_t[:, 12:18, :].rearrange("p a b -> p (a b)").rearrange("p (s c) -> p s c", c=WMIX)
        for s in range(4):
            for kc in range(8):
                mm(psW.ap[:, 0:512], xT.ap[:, kc, s * 128:(s + 1) * 128], wv0.ap[:, kc, :], kc == 0, kc == 7, [xT.r, wv0.r], [psW.r])
            for kc in range(8):
                mm(psW.ap[:, 512:768], xT.ap[:, kc, s * 128:(s + 1) * 128], wv1.ap[:, kc, 0:256], kc == 0, kc == 7, [xT.r, wv1.r], [psW.r])
            vg = rr("ta", tA)
            act(vg.ap[:, 0:512], psW.ap[:, 0:512], AF.Gelu_apprx_tanh, [psW.r], [vg.r])
            act(vg.ap[:, 512:768], psW.ap[:, 512:768], AF.Gelu_apprx_tanh, [psW.r], [vg.r])
            layer_norm_v(vg.ap[:, 0:768], WMIX, vg.r, lnv.ap[:, 0, :], lnv.ap[:, 1, :], lnv.r, v_all[:, s, :], vR, s)
        if cut <= 3:
            continue
        for s in range(4):
            for g in range(12):
                mm(psW.ap[(g % 2) * 64:(g % 2) * 64 + 64, (g // 2) * 128:(g // 2 + 1) * 128], v_all[:, s, g * 64:(g + 1) * 64], WsT.ap[:, g, :],
                   True, True, vR + [WsT.r], [psW.r], tile_position=((0, 64) if g % 2 else None))
            sv = rr("ta", tA)
            tt(sv.ap[:, 0:768], psW.ap[:, 0:768], bsT.ap.rearrange("p a b -> p (a b)"), ALU.add, [psW.r, bsT.r], [sv.r])
            tt(scr_t[:, 6:12, s * 128:(s + 1) * 128], sv.ap[:, 0:768].rearrange("p (a b) -> p a b", b=128), scr_t[:, 0:6, s * 128:(s + 1) * 128],
               ALU.mult, [sv.r] + scrR[0:6], scrR[6:12])
        if cut <= 4:
            continue
        mem_attention(0, None)
        if cut <= 5:
            continue
        out_proj_ln_ffn(0, False, G)
        if own:
            dma("sp", x1_d[t0:t0 + 512, :].rearrange("(s p) d -> p s d", p=128), xb.ap, xbR, [R_x1d], "xbst")
        if cut <= 6:
            continue
        for (c0, wd, jbase) in ((768, 512, 0), (1280, 256, 4)):
            w = wload(wpiece(b_w_in, c0, wd), wd)
            for jj in range(wd // 128):
                j = jbase + jj
                o = rr("s", psS)
                for kc in range(8):
                    mm(o.ap, w.ap[:, kc, jj * 128:(jj + 1) * 128], xT.ap[:, kc, :], kc == 0, kc == 7, [w.r, xT.r], [o.r])
                st = rr("stg", stg)
                act(st.ap, o.ap, AF.Copy, [o.r], [st.r])
                for b2 in range(2):
                    dve(lambda e, b2=b2, st=st: e.bn_stats(out=bnst[0].ap[:, b2, :], in_=st.ap[:, b2 * 256:(b2 + 1) * 256]), [st.r], [bnst[0].r])
                    dve(lambda e, b2=b2: e.bn_aggr(out=mv[0].ap, in_=bnst[0].ap[:, b2, :]), [bnst[0].r], [mv[0].r])
                    dve(lambda e, b2=b2, j=j, G=G: e.tensor_copy(out=kmacc.ap[:, j, 2 * G + b2:2 * G + b2 + 1], in_=mv[0].ap[:, 0:1]),
                        [mv[0].r], [kmacc.r])
                dma("sp", kT_d[j * 128:(j + 1) * 128, t0:t0 + 512], st.ap, [st.r], [R_kTd], "st_" + st.r.name)
        if cut <= 7:
            continue
        wv0 = wload(wpiece(b_w_in, 1536, 512), 512)
        wv1 = wload(wpiece(b_w_in, 2048, 256), 256)
        for s in range(4):
            for kc in range(8):
                mm(psW.ap[:, 0:512], xT.ap[:, kc, s * 128:(s + 1) * 128], wv0.ap[:, kc, :], kc == 0, kc == 7, [xT.r, wv0.r], [psW.r])
            for kc in range(8):
                mm(psW.ap[:, 512:768], xT.ap[:, kc, s * 128:(s + 1) * 128], wv1.ap[:, kc, 0:256], kc == 0, kc == 7, [xT.r, wv1.r], [psW.r])
            vs_ = rr("st", vst)
            for (lo, hi, p0, p1) in ((0, 512, 0, 4), (512, 768, 4, 6)):
                pv = psW.ap[:, lo:hi].rearrange("p (pr hh d) -> p pr hh d", hh=2, d=64)
                vv = vs_.ap[:, 2 * p0:2 * p1, :].rearrange("p (pr hh) c -> p pr hh c", hh=2)
                act(vv[:, :, 0, 0:64], pv[:, :, 0, :], AF.Copy, [psW.r], [vs_.r])
                act(vv[:, :, 1, 64:128], pv[:, :, 1, :], AF.Copy, [psW.r], [vs_.r])
            for q4 in range(4):
                dma("sp", v_d[3 * q4:3 * q4 + 3, :, G * 4 + s, :].rearrange("h p c -> p h c"), vs_.ap[:, 3 * q4:3 * q4 + 3, :],
                    [vs_.r], [R_vd], "st_" + vs_.r.name)
        if own:
            for (c0, wd, jbase) in ((0, 512, 0), (512, 256, 4)):
                w = wload(wpiece(b_w_in, c0, wd), wd)
                for jj in range(wd // 128):
                    j = jbase + jj
                    o = rr("s", psS)
                    for kc in range(8):
                        mm(o.ap, w.ap[:, kc, jj * 128:(jj + 1) * 128], xT.ap[:, kc, :], kc == 0, kc == 7, [w.r, xT.r], [o.r])
                    st = rr("stg", stg)
                    act(st.ap, o.ap, AF.Copy, [o.r], [st.r], scale=0.125)
                    dma("sp", qT_d[j * 128:(j + 1) * 128, t0:t0 + 512], st.ap, [st.r], [R_qTd], "st_" + st.r.name)
            w = wload(wpiece(b_w_in, 2304, 256), 256)
            for jj in range(2):
                o = rr("s", psS)
                for kc in range(8):
                    mm(o.ap, w.ap[:, kc, jj * 128:(jj + 1) * 128], xT.ap[:, kc, :], kc == 0, kc == 7, [w.r, xT.r], [o.r])
                st = rr("stg", stg)
                act(st.ap, o.ap, AF.Copy, [o.r], [st.r], scale=0.125)
                dma("sp", qmT_d[jj * 128:(jj + 1) * 128, t0:t0 + 512], st.ap, [st.r], [R_qmTd], "st_" + st.r.name)
    ts(kmbf.ap, kmacc.ap, 1.0, None, ALU.mult, None, [kmacc.r], [kmbf.r])
    dma("sp", km_d.rearrange("(j p) n -> p j n", p=128), kmbf.ap, [kmbf.r], [R_kmd], "kmst")

    P.barrier(lambda e: e.dma_start(out=bar_d, in_=negh_d))
    dma("sp", lnp.ap, lnrep_d[1], [], [lnp.r], "lnp")
    dma("sp", lncol.ap, lncol_d[1], [], [lncol.r], "lncol")
    dma("sp", pastneg.ap, pastneg_d, [], [pastneg.r], "pastneg")
    dma("sp", trim.ap, trim_d, [], [trim.r], "trim")
    dma("sp", kmT.ap, km_d.rearrange("(h d) n -> d h n", d=64), [R_kmd], [kmT.r], "kmT")
    for sg in range(NSEG):
        dma("sp", KA.ap[64:96, sg * 1024:(sg + 1) * 1024], ohA_d[:, sg * 1024:(sg + 1) * 1024], [], [KR[0][sg]], "K0_%d" % sg)
        dma("sp", KB.ap[64:96, sg * 1024:(sg + 1) * 1024], ohB_d[:, sg * 1024:(sg + 1) * 1024], [], [KR[1][sg]], "K1_%d" % sg)
    for i in range(2):
        P.op("dve", lambda e, i=i: e.memset(biasin[i].ap, 0.0), writes=[biasin[i].r])
    for p in range(ng2):
        t0 = p * 512
        nblk = 2 * p + 2
        nch = 2 * nblk
        dma("sp", xb.ap, x1_d[t0:t0 + 512, :].rearrange("(s p) d -> p s d", p=128), [R_x1d], xbR, "xb")
        for pr in range(2):
            dma("sp", scr(18 + pr), qmT_d[pr * 128:(pr + 1) * 128, t0:t0 + 512], [R_qmTd], [scrR[18 + pr]], "scr%d" % (18 + pr))
        def sel_issue(h):
            qa = Qaug[h % 2]
            bi = biasin[h % 2]
            dma("sp", qa.ap[0:64, :], qT_d[h * 64:(h + 1) * 64, t0:t0 + 512], [R_qTd], [qa.r], "qa%d" % (h % 2))
            for s in range(4):
                mm(psSC.ap[:, s, :], qa.ap[0:64, s * 128:(s + 1) * 128], kmT.ap[:, h, :], True, True, [qa.r, kmT.r], [psSC.r])
            for s in range(4):
                i_blk = 2 * p + s // 2
                tt(scm.ap[:, s, :], psSC.ap[:, s, :], pastneg.ap[:, i_blk, :], ALU.add, [psSC.r, pastneg.r], [scm.r])
            for s in range(4):
                dve(lambda e, s=s: e.max(out=m8.ap[:, s, :], in_=scm.ap[:, s, :]), [scm.r], [m8.r])
            ts(thr.ap, m8.ap[:, :, 2], -2.0e30, None, ALU.max, None, [m8.r], [thr.r])
            for s in range(4):
                ts(bi.ap[:, s, 64:96], scm.ap[:, s, :], thr.ap[:, s:s + 1], 1.0, ALU.is_ge, ALU.subtract, [scm.r, thr.r], [bi.r])

        def sel_finish(h):
            qa = Qaug[h % 2]
            bi = biasin[h % 2]
            for s in range(4):
                tr(psBT.ap[:, s * 128:(s + 1) * 128], bi.ap[:, s, :], [bi.r], [psBT.r])
            act(qa.ap[64:96, :], psBT.ap[64:96, :], AF.Copy, [psBT.r], [qa.r])

        sel_issue(0)
        sel_finish(0)
        for h in range(NH):
            pr, hh = h // 2, h % 2
            qa = Qaug[h % 2]
            for pt_, (Kb_, Vb_, koff, coff) in enumerate(((KA, VA, 0, 0), (KB, VB, 4096, 32))):
                for sg in range((nblk + 3) // 4):
                    b0, b1 = sg * 4, min(sg * 4 + 4, nblk)
                    dma("sp", Kb_.ap[0:64, b0 * 256:b1 * 256], kT_d[h * 64:(h + 1) * 64, koff + b0 * 256:koff + b1 * 256],
                        [R_kTd], [KR[pt_][sg]], "K%d_%d" % (pt_, sg))
                    dma("sp", Vb_.ap[:, 2 * b0:2 * b1, :], v_d[h, :, coff + 2 * b0:coff + 2 * b1, :],
                        [R_vd], [VR[pt_][sg]], "V%d_%d" % (pt_, sg))
            if h + 1 < NH:
                sel_issue(h + 1)
            o = rr("o", psO)
            pairs = [(Kb, Vb, part, blk) for (Kb, Vb, part) in ((KA, VA, 0), (KB, VB, 1)) for blk in range(nblk)]
            total = len(pairs)
            stiles = [(psSS_t[:, :], [psS[0].r, psS[1].r]), (psW.ap, [psW.r])]

            def emit_S(pi):
                Kb, Vb, part, blk = pairs[pi]
                st_ap, st_rs = stiles[pi % 2]
                diag = (part == 0) and (blk >= 2 * p)
                for kc in range(2):
                    cc = blk * 2 + kc
                    ksl = slice(cc * 128, (cc + 1) * 128)
                    so = st_ap[:, kc * 512:(kc + 1) * 512]
                    if not diag:
                        mm(so, Kb.ap[0:96, ksl], qa.ap[0:96, :], True, True, [KR[part][blk // 4], qa.r], st_rs)
                    else:
                        qsel = blk - 2 * p
                        for half in range(2):
                            cs = slice(half * 256, (half + 1) * 256)
                            if half == qsel:
                                mm(so[:, cs], Kb.ap[0:64, ksl], qa.ap[0:64, cs], True, False, [KR[part][blk // 4], qa.r], st_rs)
                                mm(so[:, cs], ident.ap, trim.ap[:, kc, :], False, True, [ident.r, trim.r], st_rs)
                            else:
                                mm(so[:, cs], Kb.ap[0:96, ksl], qa.ap[0:96, cs], True, True, [KR[part][blk // 4], qa.r], st_rs)

            emit_S(0)
            if total > 1:
                emit_S(1)
            for pi in range(total):
                Kb, Vb, part, blk = pairs[pi]
                st_ap, st_rs = stiles[pi % 2]
                pt = PT2[pi % 2]
                act(pt.ap, st_ap, AF.Exp, st_rs, [pt.r])
                if pi + 2 < total:
                    emit_S(pi + 2)
                for kc in range(2):
                    cc = blk * 2 + kc
                    mm(o.ap, Vb.ap[:, cc, :], pt.ap[:, kc * 512:(kc + 1) * 512], pi == 0 and kc == 0, pi == total - 1 and kc == 1,
                       [VR[part][blk // 4], pt.r], [o.r])
            if h + 1 < NH:
                sel_finish(h + 1)
            normalise_into(o, hh, 6 + pr)
        mem_attention(1, None)
        out_proj_ln_ffn(1, True, p)
        dma("sp", out_d[t0:t0 + 512, :].rearrange("(s p) d -> p s d", p=128), xb.ap, xbR, [], "xbst")

    n_inst = P.finalize_and_emit(nc)
    return nc, n_inst


_CACHE = {}


def _consts(r):
    bf = ml_dtypes.bfloat16
    ident = np.eye(128, dtype=np.float32).astype(bf)
    tril = np.tril(np.ones((128, 128), np.float32))
    kpos = np.arange(128)[:, None, None] + 128 * np.arange(2)[None, :, None]
    qpos = np.arange(256)[None, None, :]
    trimask = np.where(kpos > qpos, -NEGV, 0.0).astype(np.float32).astype(bf)
    pn = np.full((16, 32), -3.0e30, np.float32)
    for i in range(16):
        j = 2 * i + r
        for n in range(32):
            jj = 2 * n + r if n < 16 else 2 * (n - 16) + (1 - r)
            if jj < j:
                pn[i, n] = 0.0
    pastneg = np.ascontiguousarray(np.broadcast_to(pn[None], (128, 16, 32)))
    ohA = np.zeros((32, 4096), np.float32)
    ohB = np.zeros((32, 4096), np.float32)
    for n in range(16):
        ohA[n, n * 256:(n + 1) * 256] = NEGV
        ohB[16 + n, n * 256:(n + 1) * 256] = NEGV
    return dict(ident=ident, tril=tril, trimask=trimask, pastneg=pastneg, ohA=ohA.astype(bf), ohB=ohB.astype(bf),
                negh=np.full((128, 1), -0.5, np.float32))


def _in_maps(inp):
    f = lambda a: np.ascontiguousarray(np.asarray(a, dtype=np.float32))
    x = f(inp["x"])
    mem = f(inp["mem"])
    lnrep = np.stack([np.stack([np.broadcast_to(f(inp[k])[L][None, :], (128, D)) for k in ("ln_mix_g", "ln_mix_b", "ln_ffn_g", "ln_ffn_b")], 1)
                      for L in range(2)], 0)
    lnrep = np.ascontiguousarray(lnrep)
    lncol = np.ascontiguousarray(np.stack([np.stack([f(inp[k])[L].reshape(8, 128).T for k in ("ln_mix_g", "ln_mix_b", "ln_ffn_g", "ln_ffn_b")], 1)
                                           for L in range(2)], 0))
    lnv = np.ascontiguousarray(np.stack([np.broadcast_to(f(inp[k])[0][None, :], (128, WMIX)) for k in ("a_ln_v_g", "a_ln_v_b")], 1))
    bs = f(inp["a_b_s"])[0]
    bsT = np.ascontiguousarray(np.repeat(bs.reshape(6, 2, 128).transpose(1, 0, 2), 64, axis=0))
    shared = dict(a_w_in=f(inp["a_w_in"])[0], a_w_s=f(inp["a_w_s"])[0], b_w_in=f(inp["b_w_in"])[0], w_mem_kv=f(inp["w_mem_kv"]),
                  w_mix_out=f(inp["w_mix_out"]), w_ffn_in=f(inp["w_ffn_in"]), w_ffn_out=f(inp["w_ffn_out"]),
                  lnrep=lnrep, lnv=lnv, bsT=bsT, lncol=lncol)
    maps = []
    for c in range(8):
        b, r = c // 2, c % 2
        xb_ = x[b].reshape(16, 2, 256, D)
        xl = np.ascontiguousarray(np.concatenate([xb_[:, r], xb_[:, 1 - r]], 0).reshape(SEQ, D))
        m = dict(shared)
        m.update(_consts(r))
        m["x"] = xl
        m["mem"] = np.ascontiguousarray(mem[b])
        maps.append(m)
    return maps


def kernel(**inputs):
    debug = bool(os.environ.get("MK_DEBUG"))
    key = ("nc", debug)
    if key not in _CACHE:
        _CACHE[key] = build_program(debug, int(os.environ.get("MK_NG1", NG1)), int(os.environ.get("MK_NG2", NG2)), int(os.environ.get("MK_CUT", 99)))
    nc, n_inst = _CACHE[key]
    maps = _in_maps(inputs)
    res = run_bass_kernel_spmd(nc, maps, core_ids=list(range(8)))
    out = np.empty((4, SEQ, D), np.float32)
    for c in range(8):
        b, r = c // 2, c % 2
        o = np.asarray(res.results[c]["out"], dtype=np.float32).reshape(16, 256, D)
        out[b].reshape(16, 2, 256, D)[:, r] = o
    if debug:
        kernel.debug = [np.asarray(res.results[c]["x1dbg"]) for c in range(8)]
    return out
```

```python
import os
import numpy as np
import ml_dtypes
import concourse.bass as bass
import concourse.mybir as mybir
from concourse.bass_utils import run_bass_kernel_spmd

F32 = mybir.dt.float32
BF16 = mybir.dt.bfloat16
AF = mybir.ActivationFunctionType
ALU = mybir.AluOpType
AX = mybir.AxisListType

ENGS = ("pe", "act", "dve", "pool", "sp")

D = 1024
SEQ = 8192
NH = 12
DH = 64
WMIX = 768
NFF = 22
DFF = 2816
ALPHA = 4.0 ** 0.25
LN_EPS = 1e-5
NEGV = 30000.0
NG1 = 16
NG2 = 8


class Res:
    __slots__ = ("name", "last_w", "readers")

    def __init__(self, name):
        self.name = name
        self.last_w = None
        self.readers = []


class Op:
    __slots__ = ("eng", "fn", "deps", "is_dma", "dsem", "token", "signal", "idx", "inc")

    def __init__(self, eng, fn, is_dma=False, dsem=None):
        self.eng = eng
        self.fn = fn
        self.deps = []
        self.is_dma = is_dma
        self.dsem = dsem
        self.token = None
        self.signal = is_dma
        self.idx = None
        self.inc = 16


class Prog:
    def __init__(self):
        self.ops = []
        self.dma_counts = {}
        self.last_dma = {}
        self.last_eng = {}
        self.barrier_op = None

    def op(self, eng, fn, reads=(), writes=(), is_dma=False, dsem=None, inc=16, extra_deps=()):
        o = Op(eng, fn, is_dma, dsem)
        o.inc = inc
        o.idx = len(self.ops)
        deps = {}
        for r in reads:
            w = r.last_w
            if w is not None:
                deps[w.idx] = w
        for r in writes:
            w = r.last_w
            if w is not None and (w.is_dma or is_dma or w.eng != eng):
                if not (w.is_dma and is_dma and w.dsem == dsem):
                    deps[w.idx] = w
            for rd in r.readers:
                if rd.is_dma or is_dma or rd.eng != eng:
                    deps[rd.idx] = rd
        for p in extra_deps:
            deps[p.idx] = p
        if self.barrier_op is not None:
            deps[self.barrier_op.idx] = self.barrier_op
        for r in reads:
            r.readers.append(o)
        for r in writes:
            r.last_w = o
            r.readers = []
        deps.pop(o.idx, None)
        o.deps = list(deps.values())
        for p in o.deps:
            p.signal = True
        if is_dma:
            c = self.dma_counts.get(dsem, 0) + inc
            self.dma_counts[dsem] = c
            o.token = (("dma", dsem), c)
            self.last_dma[dsem] = o
        else:
            self.last_eng[eng] = o
        self.ops.append(o)
        return o

    def barrier(self, fn):
        deps = list(self.last_dma.values()) + list(self.last_eng.values())
        self.barrier_op = None
        b = self.op("sp", fn, is_dma=True, dsem="barrier%d" % len(self.ops), extra_deps=deps)
        self.barrier_op = b
        return b

    def finalize_and_emit(self, nc, final_wait_eng="sp"):
        cnt = {e: 0 for e in ENGS}
        for o in self.ops:
            if not o.is_dma and o.signal:
                cnt[o.eng] += 1
                o.token = (("eng", o.eng), cnt[o.eng])
        sem_keys = [("eng", e) for e in ENGS] + [("dma", k) for k in self.dma_counts]
        sems = {}
        for k in sem_keys:
            sems[k] = nc.alloc_semaphore(name="s_%s_%s" % (k[0], str(k[1])))
        seen = {e: {} for e in ENGS}
        per_eng = {e: [] for e in ENGS}
        for o in self.ops:
            mw = {}
            for p in o.deps:
                key, val = p.token
                if seen[o.eng].get(key, 0) >= val:
                    continue
                seen[o.eng][key] = val
                mw[key] = max(mw.get(key, 0), val)
            per_eng[o.eng].append((o, list(mw.items())))
        finals = [(("dma", k), c) for k, c in self.dma_counts.items()]
        n_inst = {e: 0 for e in ENGS}
        with nc.Block() as block:
            def run(engname, engobj):
                for o, waits in per_eng[engname]:
                    for k, v in waits:
                        engobj.wait_ge(sems[k], v)
                        n_inst[engname] += 1
                    ins = o.fn(engobj)
                    n_inst[engname] += 1
                    if o.is_dma:
                        ins.then_inc(sems[o.token[0]], o.inc)
                    elif o.signal:
                        ins.then_inc(sems[o.token[0]], 1)
                if engname == final_wait_eng:
                    for k, v in finals:
                        if seen[engname].get(k, 0) < v:
                            engobj.wait_ge(sems[k], v)

            @block.tensor
            def _(e):
                run("pe", e)

            @block.scalar
            def _(e):
                run("act", e)

            @block.vector
            def _(e):
                run("dve", e)

            @block.gpsimd
            def _(e):
                run("pool", e)

            @block.sync
            def _(e):
                run("sp", e)
        return n_inst


class Buf:
    __slots__ = ("ap", "r")

    def __init__(self, ap, name):
        self.ap = ap
        self.r = Res(name)


def build_program(debug=False, ng1=NG1, ng2=NG2, cut=99):
    nc = bass.Bass("TRN2", target_bir_lowering=False)
    P = Prog()

    def din(name, shape, dt=F32):
        return nc.dram_tensor(name, list(shape), dt, kind="ExternalInput").ap()

    x_d = din("x", [SEQ, D])
    mem_d = din("mem", [256, D])
    a_w_in = din("a_w_in", [D, 1792])
    a_w_s = din("a_w_s", [12, 128, 128])
    b_w_in = din("b_w_in", [D, 2560])
    w_mem_kv = din("w_mem_kv", [2, D, 512])
    w_mix_out = din("w_mix_out", [2, D, D])
    w_ffn_in = din("w_ffn_in", [2, D, 2 * DFF])
    w_ffn_out = din("w_ffn_out", [2, DFF, D])
    lnrep_d = din("lnrep", [2, 128, 4, D])
    lnv_d = din("lnv", [128, 2, WMIX])
    lncol_d = din("lncol", [2, 128, 4, 8])
    bsT_d = din("bsT", [128, 6, 128])
    negh_d = din("negh", [128, 1])
    ident_d = din("ident", [128, 128], BF16)
    tril_d = din("tril", [128, 128])
    trim_d = din("trimask", [128, 2, 256], BF16)
    pastneg_d = din("pastneg", [128, 16, 32])
    ohA_d = din("ohA", [32, 4096], BF16)
    ohB_d = din("ohB", [32, 4096], BF16)
    out_d = nc.dram_tensor("out", [4096, D], F32, kind="ExternalOutput").ap()
    def dscr(name, shape, dt):
        return nc.dram_tensor(name, list(shape), dt, kind="ExternalOutput").ap()

    x1_d = dscr("x1dbg", [4096, D], F32)
    qT_d = dscr("qTs", [768, 4096], BF16)
    qmT_d = dscr("qmTs", [256, 4096], BF16)
    kT_d = dscr("kTs", [768, SEQ], BF16)
    v_d = dscr("vs", [12, 128, 64, 128], BF16)
    km_d = dscr("kms", [768, 32], BF16)
    bar_d = nc.dram_tensor("bars", [128, 1], F32).ap()
    R_x1d, R_qTd, R_qmTd, R_kTd, R_vd, R_kmd = (Res(n) for n in ("x1d", "qTd", "qmTd", "kTd", "vd", "kmd"))

    def sb(name, shape, dt):
        return Buf(nc.alloc_sbuf_tensor("sb_" + name, list(shape), dt)[:], name)

    def ps(name, shape, dt=F32):
        return Buf(nc.alloc_psum_tensor(name, list(shape), dt)[:], name)

    xb = sb("xb", [128, 4, D], F32)
    xbR = [Res("xb%d" % i) for i in range(4)]
    xT = sb("xT", [128, 8, 512], BF16)
    xT2 = sb("xT2", [128, 8, 512], BF16)
    scr_t = nc.alloc_sbuf_tensor("sb_scr", [128, NFF, 512], BF16)
    scrR = [Res("scr%d" % i) for i in range(NFF)]
    wout_t = nc.alloc_sbuf_tensor("sb_wout", [128, NFF, D], BF16)
    woutR = [Res("wout%d" % i) for i in range(11)]
    wsl = [sb("wsl%d" % i, [128, 8, 512], BF16) for i in range(3)]
    lnp = sb("lnp", [128, 4, D], F32)
    lncol = sb("lncol", [128, 4, 8], F32)
    ident = sb("ident", [128, 128], BF16)
    kmemT = sb("kmemT", [128, 4, 256], BF16)
    vmem = sb("vmem", [128, 16, 128], BF16)
    tmpbf = [sb("tmpbf%d" % i, [128, D], BF16) for i in range(4)]
    tbq = tmpbf
    tA = [sb("tA%d" % i, [128, D], F32) for i in range(2)]
    PT2 = [sb("PT2_%d" % i, [128, 1024], BF16) for i in range(2)]
    PT = PT2
    negh = sb("negh", [128, 1], F32)
    epsb = sb("epsb", [128, 1], F32)
    bnst = [sb("bnst%d" % i, [128, 2, 6], F32) for i in range(4)]
    mv = [sb("mv%d" % i, [128, 2], F32) for i in range(4)]
    rstd = [sb("rstd%d" % i, [128, 1], F32) for i in range(4)]
    nbias = [sb("nbias%d" % i, [128, 1], F32) for i in range(4)]
    UNI = 40 * 1024
    uni_t = nc.alloc_sbuf_tensor("sb_uni", [128, UNI // 2], BF16)

    class Carver:
        def __init__(self):
            self.off = 0

        def take(self, name, shape, dt, parts=128):
            n = int(np.prod(shape[1:]))
            nb = n * (4 if dt == F32 else 2)
            nb_al = (nb + 63) // 64 * 64
            a = uni_t[0:parts, self.off // 2:(self.off + nb) // 2]
            if dt == F32:
                a = a.bitcast(F32)
            if len(shape) == 3:
                a = a.rearrange("p (a b) -> p a b", b=shape[2])
            elif len(shape) == 4:
                a = a.rearrange("p (a b c) -> p a b c", b=shape[2], c=shape[3])
            self.off += nb_al
            assert self.off <= UNI, (name, self.off)
            return Buf(a, name)

    c1 = Carver()
    vst = [c1.take("vst%d" % i, [128, 12, 128], BF16) for i in range(2)]
    stg = [c1.take("stg%d" % i, [128, 512], BF16) for i in range(3)]
    lnv = c1.take("lnv", [128, 2, WMIX], F32)
    bsT = c1.take("bsT", [128, 6, 128], F32)
    WsT = c1.take("WsT", [128, 12, 128], BF16)
    kmacc = c1.take("kmacc", [128, 6, 32], F32)
    kmbf = c1.take("kmbf", [128, 6, 32], BF16)
    c2 = Carver()
    KA = c2.take("KA", [96, 4096], BF16, parts=96)
    KB = c2.take("KB", [96, 4096], BF16, parts=96)
    VA = c2.take("VA", [128, 32, 128], BF16)
    VB = c2.take("VB", [128, 32, 128], BF16)
    Qaug = [c2.take("Qaug%d" % i, [96, 512], BF16, parts=96) for i in range(2)]
    pastneg = c2.take("pastneg", [128, 16, 32], F32)
    trim = c2.take("trim", [128, 2, 256], BF16)
    kmT = c2.take("kmT", [64, 12, 32], BF16, parts=64)
    scm = c2.take("scm", [128, 4, 32], F32)
    m8 = c2.take("m8", [128, 4, 8], F32)
    thr = c2.take("thr", [128, 4], F32)
    biasin = [c2.take("biasin%d" % i, [128, 4, 96], BF16) for i in range(2)]
    NSEG = 4
    KR = [[Res("K%d_%d" % (pt_, sg)) for sg in range(NSEG)] for pt_ in range(2)]
    VR = [[Res("V%d_%d" % (pt_, sg)) for sg in range(NSEG)] for pt_ in range(2)]

    psSS_t = nc.alloc_psum_tensor("psSS", [128, 1024], F32)
    psS = [Buf(psSS_t[:, 512 * i:512 * (i + 1)], "psS%d" % i) for i in range(2)]
    psO = [ps("psO%d" % i, [128, 512]) for i in range(2)]
    psW = ps("psW", [128, 1024])
    psM_t = nc.alloc_psum_tensor("psM", [128, 1024], BF16)
    psSC = Buf(psM_t[:, 0:256].bitcast(F32).rearrange("p (a b) -> p a b", b=32), "psSC")
    psBT = Buf(psM_t[0:96, 512:1024], "psBT")
    psBT.r = psSC.r
    psT = ps("psT", [128, 1024], BF16)

    cnt = {"s": 0, "o": 0, "w": 0, "pt": 0, "stg": 0, "tb": 0, "ta": 0, "st": 0}

    def rr(key, lst):
        i = cnt[key]
        cnt[key] = i + 1
        return lst[i % len(lst)]

    def dma(q, out, in_, reads, writes, dsem):
        return P.op(q, lambda e: e.dma_start(out=out, in_=in_), reads=reads, writes=writes, is_dma=True, dsem=dsem)

    def mm(out, lhsT, rhs, start, stop, reads, writes, tile_position=None):
        if tile_position is None:
            return P.op("pe", lambda e: e.matmul(out, lhsT=lhsT, rhs=rhs, start=start, stop=stop), reads=reads, writes=writes)
        return P.op("pe", lambda e: e.matmul(out, lhsT=lhsT, rhs=rhs, start=start, stop=stop, tile_position=tile_position),
                    reads=reads, writes=writes)

    def tr(out, in_, reads, writes):
        n = in_.shape[0]
        return P.op("pe", lambda e: e.transpose(out=out, in_=in_, identity=ident.ap[0:n, 0:n]), reads=list(reads) + [ident.r], writes=writes)

    def act(out, in_, func, reads, writes, scale=1.0, bias=0.0):
        return P.op("act", lambda e: e.activation(out=out, in_=in_, func=func, bias=bias, scale=scale), reads=reads, writes=writes)

    def dve(fn, reads, writes):
        return P.op("dve", fn, reads=reads, writes=writes)

    def tt(out, in0, in1, op, reads, writes, eng="dve"):
        return P.op(eng, lambda e: e.tensor_tensor(out=out, in0=in0, in1=in1, op=op), reads=reads, writes=writes)

    def ts(out, in0, s1, s2, op0, op1, reads, writes):
        if s2 is None:
            return P.op("dve", lambda e: e.tensor_scalar(out=out, in0=in0, scalar1=s1, scalar2=None, op0=op0), reads=reads, writes=writes)
        return P.op("dve", lambda e: e.tensor_scalar(out=out, in0=in0, scalar1=s1, scalar2=s2, op0=op0, op1=op1), reads=reads, writes=writes)

    wcnt = [0]

    def wload(src3, width):
        s = wsl[wcnt[0] % 3]
        wcnt[0] += 1
        dma("pool", s.ap[:, :, 0:width], src3, [], [s.r], "wsl_" + s.r.name)
        return s

    def wpiece(w2d, c0, width):
        return w2d.rearrange("(kc p) c -> p kc c", p=128)[:, :, c0:c0 + width]

    def scr(i):
        return scr_t[:, i, :]

    dma("sp", ident.ap, ident_d, [], [ident.r], "ident")
    dma("sp", negh.ap, negh_d, [], [negh.r], "negh")
    P.op("dve", lambda e: e.memset(epsb.ap, LN_EPS), writes=[epsb.r])
    dma("sp", lnv.ap, lnv_d, [], [lnv.r], "lnv")
    dma("sp", bsT.ap, bsT_d, [], [bsT.r], "bsT")
    dma("sp", lnp.ap, lnrep_d[0], [], [lnp.r], "lnp")
    dma("sp", lncol.ap, lncol_d[0], [], [lncol.r], "lncol")
    wsf = tA[0]
    trl = tA[1]
    wsm = tmpbf[0]
    dma("sp", trl.ap[:, 0:128], tril_d, [], [trl.r], "tA1")
    for half in range(2):
        dma("sp", wsf.ap[:, 0:768].rearrange("p (g s) -> p g s", s=128), a_w_s[6 * half:6 * half + 6].rearrange("g t s -> t g s"),
            [], [wsf.r], "tA0")
        for gl in range(6):
            tt(wsm.ap[:, gl * 128:(gl + 1) * 128], wsf.ap[:, gl * 128:(gl + 1) * 128], trl.ap[:, 0:128], ALU.mult,
               [wsf.r, trl.r], [wsm.r])
        for gl in range(6):
            tr(psT.ap[:, gl * 128:(gl + 1) * 128], wsm.ap[:, gl * 128:(gl + 1) * 128], [wsm.r], [psT.r])
        dve(lambda e, half=half: e.tensor_copy(out=WsT.ap[:, half * 6:(half + 1) * 6, :],
                                               in_=psT.ap[:, 0:768].rearrange("p (g t) -> p g t", t=128)),
            [psT.r], [WsT.r])

    memT = xT2
    for mc in range(2):
        dma("sp", tA[mc].ap, mem_d[mc * 128:(mc + 1) * 128, :], [], [tA[mc].r], "tA%d" % mc)
        act(tmpbf[mc].ap, tA[mc].ap, AF.Copy, [tA[mc].r], [tmpbf[mc].r])
        for kc in range(8):
            tr(psT.ap[:, kc * 128:(kc + 1) * 128], tmpbf[mc].ap[:, kc * 128:(kc + 1) * 128], [tmpbf[mc].r], [psT.r])
        dve(lambda e, mc=mc: e.tensor_copy(out=memT.ap[:, :, mc * 128:(mc + 1) * 128],
                                           in_=psT.ap.rearrange("p (k t) -> p k t", t=128)), [psT.r], [memT.r])
    P.op("dve", lambda e: e.memset(vmem.ap, 1.0), writes=[vmem.r])
    for L in range(2):
        w = wload(wpiece(w_mem_kv[L], 0, 512), 512)
        for pr in range(2):
            o = rr("s", psS)
            for kc in range(8):
                mm(o.ap[:, 0:256], w.ap[:, kc, pr * 128:(pr + 1) * 128], memT.ap[:, kc, 0:256], kc == 0, kc == 7, [w.r, memT.r], [o.r])
            act(kmemT.ap[:, L * 2 + pr, :], o.ap[:, 0:256], AF.Copy, [o.r], [kmemT.r])
        for mc in range(2):
            o = rr("s", psS)
            for kc in range(8):
                mm(o.ap[:, 0:256], memT.ap[:, kc, mc * 128:(mc + 1) * 128], w.ap[:, kc, 256:512], kc == 0, kc == 7, [w.r, memT.r], [o.r])
            ov = o.ap[:, 0:256].rearrange("p (pr hh d) -> p pr hh d", hh=2, d=64)
            vv = vmem.ap[:, (L * 2 + mc) * 4:(L * 2 + mc) * 4 + 4, :].rearrange("p (pr hh) c -> p pr hh c", hh=2)
            act(vv[:, :, 0, 0:64], ov[:, :, 0, :], AF.Copy, [o.r], [vmem.r])
            act(vv[:, :, 1, 64:128], ov[:, :, 1, :], AF.Copy, [o.r], [vmem.r])

    def transposes_to(dst, s, srcbuf, vi=None):
        for kc in range(8):
            tr(psT.ap[:, kc * 128:(kc + 1) * 128], srcbuf.ap[:, kc * 128:(kc + 1) * 128], [srcbuf.r], [psT.r])
        if vi is None:
            act(dst.ap[:, :, s * 128:(s + 1) * 128], psT.ap.rearrange("p (k t) -> p k t", t=128), AF.Copy, [psT.r], [dst.r])
        else:
            for kc in range(8):
                P.op("act", lambda e, kc=kc: e.activation(out=dst.ap[:, kc, s * 128:(s + 1) * 128], in_=psT.ap[:, kc * 128:(kc + 1) * 128],
                                                         func=AF.Identity, bias=lncol.ap[:, vi + 1, kc:kc + 1], scale=lncol.ap[:, vi, kc:kc + 1]),
                     reads=[psT.r, lncol.r], writes=[dst.r])

    def ln_stats(buf_ap, width, bufR, k):
        nchunk = 2
        cw = width // nchunk
        for c in range(nchunk):
            dve(lambda e, c=c: e.bn_stats(out=bnst[k].ap[:, c, :], in_=buf_ap[:, c * cw:(c + 1) * cw]), [bufR], [bnst[k].r])
        dve(lambda e: e.bn_aggr(out=mv[k].ap, in_=bnst[k].ap.rearrange("p a b -> p (a b)")), [bnst[k].r], [mv[k].r])
        act(nbias[k].ap, mv[k].ap[:, 1:2], AF.Sqrt, [mv[k].r, epsb.r], [nbias[k].r], bias=epsb.ap)
        dve(lambda e: e.reciprocal(out=rstd[k].ap, in_=nbias[k].ap), [nbias[k].r], [rstd[k].r])
        ts(nbias[k].ap, mv[k].ap[:, 0:1], -1.0, rstd[k].ap, ALU.mult, ALU.mult, [mv[k].r, rstd[k].r], [nbias[k].r])

    def layer_norm_v(buf_ap, width, bufR, gam, bet, gbR, outbf_ap, outbfRs, k):
        ln_stats(buf_ap, width, bufR, k)
        P.op("act", lambda e: e.activation(out=buf_ap, in_=buf_ap, func=AF.Identity, bias=nbias[k].ap, scale=rstd[k].ap),
             reads=[bufR, nbias[k].r, rstd[k].r], writes=[bufR])
        tt(buf_ap, buf_ap, gam, ALU.mult, [bufR, gbR], [bufR])
        tt(outbf_ap, buf_ap, bet, ALU.add, [bufR, gbR], list(outbfRs))

    def layer_norm_res(buf_ap, bufR, gi, tb, k):
        ln_stats(buf_ap, D, bufR, k)
        if tb is not None:
            P.op("act", lambda e: e.activation(out=tb.ap, in_=buf_ap, func=AF.Identity, bias=nbias[k].ap, scale=rstd[k].ap),
                 reads=[bufR, nbias[k].r, rstd[k].r], writes=[tb.r])

    def layer_norm_res_finish(buf_ap, bufR, gi, k):
        ts(buf_ap, buf_ap, mv[k].ap[:, 0:1], rstd[k].ap, ALU.subtract, ALU.mult, [bufR, mv[k].r, rstd[k].r], [bufR])
        tt(buf_ap, buf_ap, lnp.ap[:, gi, :], ALU.mult, [bufR, lnp.r], [bufR])
        tt(buf_ap, buf_ap, lnp.ap[:, gi + 1, :], ALU.add, [bufR, lnp.r], [bufR])

    def mem_attention(L, qmR):
        for h in range(4):
            pr, hh = h // 2, h % 2
            pb = 64 * hh
            o = rr("o", psO)
            for mc in range(2):
                s_ = rr("s", psS)
                mm(s_.ap, kmemT.ap[pb:pb + 64, L * 2 + pr, mc * 128:(mc + 1) * 128], scr_t[pb:pb + 64, 18 + pr, :], True, True,
                   [kmemT.r, scrR[18 + pr]], [s_.r])
                pt = rr("pt", PT)
                act(pt.ap[:, 0:512], s_.ap, AF.Exp, [s_.r], [pt.r])
                mm(o.ap, vmem.ap[:, (L * 2 + mc) * 4 + h, :], pt.ap[:, 0:512], mc == 0, mc == 1, [vmem.r, pt.r], [o.r])
            normalise_into(o, hh, 20 + pr)

    def normalise_into(o, hh, scr_idx):
        ob, db = (0, 64) if hh == 0 else (64, 0)
        rec = rr("ta", tA)
        dve(lambda e: e.reciprocal(out=rec.ap[db:db + 64, 0:512], in_=o.ap[db:db + 64, :]), [o.r], [rec.r])
        tt(scr_t[ob:ob + 64, scr_idx, :], o.ap[ob:ob + 64, :], rec.ap[db:db + 64, 0:512], ALU.mult, [o.r, rec.r], [scrR[scr_idx]])

    def out_proj_ln_ffn(L, last, G):
        wo = [wload(wpiece(w_mix_out[L], hf * 512, 512), 512) for hf in range(2)]
        for j in range(11):
            dma("pool", wout_t[:, 2 * j:2 * j + 2, :], w_ffn_out[L, 256 * j:256 * (j + 1), :].rearrange("(f p) c -> p f c", p=128),
                [], [woutR[j]], "wout%d" % j)
        for s in range(4):
            for hf in range(2):
                y = rr("o", psO)
                for c in range(8):
                    si = 6 + c if c < 6 else 20 + (c - 6)
                    mm(y.ap, scr_t[:, si, s * 128:(s + 1) * 128], wo[hf].ap[:, c, :], c == 0, c == 7, [scrR[si], wo[hf].r], [y.r])
                xs = xb.ap[:, s, hf * 512:(hf + 1) * 512]
                dve(lambda e, xs=xs, y=y: e.scalar_tensor_tensor(out=xs, in0=xs, scalar=ALPHA, in1=y.ap, op0=ALU.mult, op1=ALU.add),
                    [xbR[s], y.r], [xbR[s]])
            layer_norm_res(xb.ap[:, s, :], xbR[s], 0, tbq[s], s)
        for s in range(4):
            transposes_to(xT2, s, tbq[s], vi=0)
        for s in range(4):
            layer_norm_res_finish(xb.ap[:, s, :], xbR[s], 0, s)
        for j in range(11):
            w = wsl[wcnt[0] % 3]
            wcnt[0] += 1
            for two in range(2):
                dma("pool", w.ap[:, :, 256 * two:256 * (two + 1)], wpiece(w_ffn_in[L], two * DFF + 256 * j, 256), [], [w.r], "wsl_" + w.r.name)
            for fl in range(2):
                f = 2 * j + fl
                pg = rr("s", psS)
                pu = rr("o", psO)
                for kc in range(8):
                    mm(pg.ap, w.ap[:, kc, fl * 128:(fl + 1) * 128], xT2.ap[:, kc, :], kc == 0, kc == 7, [w.r, xT2.r], [pg.r])
                for kc in range(8):
                    mm(pu.ap, w.ap[:, kc, 256 + fl * 128:256 + (fl + 1) * 128], xT2.ap[:, kc, :], kc == 0, kc == 7, [w.r, xT2.r], [pu.r])
                sg = rr("ta", tA)
                act(sg.ap[:, 0:512], pg.ap, AF.Silu, [pg.r], [sg.r])
                tt(scr(f), sg.ap[:, 0:512], pu.ap, ALU.mult, [sg.r, pu.r], [scrR[f]])
        for s in range(4):
            for hf in range(2):
                y = rr("o", psO)
                for f in range(NFF):
                    mm(y.ap, scr_t[:, f, s * 128:(s + 1) * 128], wout_t[:, f, hf * 512:(hf + 1) * 512], f == 0, f == NFF - 1,
                       [scrR[f], woutR[f // 2]], [y.r])
                xs = xb.ap[:, s, hf * 512:(hf + 1) * 512]
                dve(lambda e, xs=xs, y=y: e.scalar_tensor_tensor(out=xs, in0=xs, scalar=ALPHA, in1=y.ap, op0=ALU.mult, op1=ALU.add),
                    [xbR[s], y.r], [xbR[s]])
            layer_norm_res(xb.ap[:, s, :], xbR[s], 2, None if last else tbq[s], s)
        if not last:
            for s in range(4):
                transposes_to(xT, s, tbq[s], vi=2)
        for s in range(4):
            layer_norm_res_finish(xb.ap[:, s, :], xbR[s], 2, s)

    P.op("dve", lambda e: e.memset(kmacc.ap, 0.0), writes=[kmacc.r])
    for i in range(2):
        P.op("dve", lambda e, i=i: e.memset(vst[i].ap, 1.0), writes=[vst[i].r])
    for G in range(ng1):
        own = G < NG2
        t0 = G * 512
        dma("sp", xb.ap, x_d[t0:t0 + 512, :].rearrange("(s p) d -> p s d", p=128), [], xbR, "xb")
        for s in range(4):
            tb = rr("tb", tmpbf)
            act(tb.ap, xb.ap[:, s, :], AF.Copy, [xbR[s]], [tb.r])
            transposes_to(xT, s, tb)
        if cut <= 1:
            continue
        for (c0, wd, base) in ((0, 512, 0), (512, 256, 4)):
            w = wload(wpiece(a_w_in, c0, wd), wd)
            for jj in range(wd // 128):
                o = rr("s", psS)
                for kc in range(8):
                    mm(o.ap, w.ap[:, kc, jj * 128:(jj + 1) * 128], xT.ap[:, kc, :], kc == 0, kc == 7, [w.r, xT.r], [o.r])
                act(scr(base + jj), o.ap, AF.Gelu_apprx_tanh, [o.r], [scrR[base + jj]])
        w = wload(wpiece(a_w_in, 1536, 256), 256)
        for jj in range(2):
            o = rr("s", psS)
            for kc in range(8):
                mm(o.ap, w.ap[:, kc, jj * 128:(jj + 1) * 128], xT.ap[:, kc, :], kc == 0, kc == 7, [w.r, xT.r], [o.r])
            act(scr(18 + jj), o.ap, AF.Copy, [o.r], [scrR[18 + jj]], scale=0.125)
        if cut <= 2:
            continue
        wv0 = wload(wpiece(a_w_in, 768, 512), 512)
        wv1 = wload(wpiece(a_w_in, 1280, 256), 256)
        vR = scrR[12:18]
        v_all = scr_t[:, 12:18, :].rearrange("p a b -> p (a b)").rearrange("p (s c) -> p s c", c=WMIX)
        for s in range(4):
            for kc in range(8):
                mm(psW.ap[:, 0:512], xT.ap[:, kc, s * 128:(s + 1) * 128], wv0.ap[:, kc, :], kc == 0, kc == 7, [xT.r, wv0.r], [psW.r])
            for kc in range(8):
                mm(psW.ap[:, 512:768], xT.ap[:, kc, s * 128:(s + 1) * 128], wv1.ap[:, kc, 0:256], kc == 0, kc == 7, [xT.r, wv1.r], [psW.r])
            vg = rr("ta", tA)
            act(vg.ap[:, 0:512], psW.ap[:, 0:512], AF.Gelu_apprx_tanh, [psW.r], [vg.r])
            act(vg.ap[:, 512:768], psW.ap[:, 512:768], AF.Gelu_apprx_tanh, [psW.r], [vg.r])
            layer_norm_v(vg.ap[:, 0:768], WMIX, vg.r, lnv.ap[:, 0, :], lnv.ap[:, 1, :], lnv.r, v_all[:, s, :], vR, s)
        if cut <= 3:
            continue
        for s in range(4):
            for g in range(12):
                mm(psW.ap[(g % 2) * 64:(g % 2) * 64 + 64, (g // 2) * 128:(g // 2 + 1) * 128], v_all[:, s, g * 64:(g + 1) * 64], WsT.ap[:, g, :],
                   True, True, vR + [WsT.r], [psW.r], tile_position=((0, 64) if g % 2 else None))
            sv = rr("ta", tA)
            tt(sv.ap[:, 0:768], psW.ap[:, 0:768], bsT.ap.rearrange("p a b -> p (a b)"), ALU.add, [psW.r, bsT.r], [sv.r])
            tt(scr_t[:, 6:12, s * 128:(s + 1) * 128], sv.ap[:, 0:768].rearrange("p (a b) -> p a b", b=128), scr_t[:, 0:6, s * 128:(s + 1) * 128],
               ALU.mult, [sv.r] + scrR[0:6], scrR[6:12])
        if cut <= 4:
            continue
        mem_attention(0, None)
        if cut <= 5:
            continue
        out_proj_ln_ffn(0, False, G)
        if own:
            dma("sp", x1_d[t0:t0 + 512, :].rearrange("(s p) d -> p s d", p=128), xb.ap, xbR, [R_x1d], "xbst")
        if cut <= 6:
            continue
        for (c0, wd, jbase) in ((768, 512, 0), (1280, 256, 4)):
            w = wload(wpiece(b_w_in, c0, wd), wd)
            for jj in range(wd // 128):
                j = jbase + jj
                o = rr("s", psS)
                for kc in range(8):
                    mm(o.ap, w.ap[:, kc, jj * 128:(jj + 1) * 128], xT.ap[:, kc, :], kc == 0, kc == 7, [w.r, xT.r], [o.r])
                st = rr("stg", stg)
                act(st.ap, o.ap, AF.Copy, [o.r], [st.r])
                for b2 in range(2):
                    dve(lambda e, b2=b2, st=st: e.bn_stats(out=bnst[0].ap[:, b2, :], in_=st.ap[:, b2 * 256:(b2 + 1) * 256]), [st.r], [bnst[0].r])
                    dve(lambda e, b2=b2: e.bn_aggr(out=mv[0].ap, in_=bnst[0].ap[:, b2, :]), [bnst[0].r], [mv[0].r])
                    dve(lambda e, b2=b2, j=j, G=G: e.tensor_copy(out=kmacc.ap[:, j, 2 * G + b2:2 * G + b2 + 1], in_=mv[0].ap[:, 0:1]),
                        [mv[0].r], [kmacc.r])
                dma("sp", kT_d[j * 128:(j + 1) * 128, t0:t0 + 512], st.ap, [st.r], [R_kTd], "st_" + st.r.name)
        if cut <= 7:
            continue
        wv0 = wload(wpiece(b_w_in, 1536, 512), 512)
        wv1 = wload(wpiece(b_w_in, 2048, 256), 256)
        for s in range(4):
            for kc in range(8):
                mm(psW.ap[:, 0:512], xT.ap[:, kc, s * 128:(s + 1) * 128], wv0.ap[:, kc, :], kc == 0, kc == 7, [xT.r, wv0.r], [psW.r])
            for kc in range(8):
                mm(psW.ap[:, 512:768], xT.ap[:, kc, s * 128:(s + 1) * 128], wv1.ap[:, kc, 0:256], kc == 0, kc == 7, [xT.r, wv1.r], [psW.r])
            vs_ = rr("st", vst)
            for (lo, hi, p0, p1) in ((0, 512, 0, 4), (512, 768, 4, 6)):
                pv = psW.ap[:, lo:hi].rearrange("p (pr hh d) -> p pr hh d", hh=2, d=64)
                vv = vs_.ap[:, 2 * p0:2 * p1, :].rearrange("p (pr hh) c -> p pr hh c", hh=2)
                act(vv[:, :, 0, 0:64], pv[:, :, 0, :], AF.Copy, [psW.r], [vs_.r])
                act(vv[:, :, 1, 64:128], pv[:, :, 1, :], AF.Copy, [psW.r], [vs_.r])
            for q4 in range(4):
                dma("sp", v_d[3 * q4:3 * q4 + 3, :, G * 4 + s, :].rearrange("h p c -> p h c"), vs_.ap[:, 3 * q4:3 * q4 + 3, :],
                    [vs_.r], [R_vd], "st_" + vs_.r.name)
        if own:
            for (c0, wd, jbase) in ((0, 512, 0), (512, 256, 4)):
                w = wload(wpiece(b_w_in, c0, wd), wd)
                for jj in range(wd // 128):
                    j = jbase + jj
                    o = rr("s", psS)
                    for kc in range(8):
                        mm(o.ap, w.ap[:, kc, jj * 128:(jj + 1) * 128], xT.ap[:, kc, :], kc == 0, kc == 7, [w.r, xT.r], [o.r])
                    st = rr("stg", stg)
                    act(st.ap, o.ap, AF.Copy, [o.r], [st.r], scale=0.125)
                    dma("sp", qT_d[j * 128:(j + 1) * 128, t0:t0 + 512], st.ap, [st.r], [R_qTd], "st_" + st.r.name)
            w = wload(wpiece(b_w_in, 2304, 256), 256)
            for jj in range(2):
                o = rr("s", psS)
                for kc in range(8):
                    mm(o.ap, w.ap[:, kc, jj * 128:(jj + 1) * 128], xT.ap[:, kc, :], kc == 0, kc == 7, [w.r, xT.r], [o.r])
                st = rr("stg", stg)
                act(st.ap, o.ap, AF.Copy, [o.r], [st.r], scale=0.125)
                dma("sp", qmT_d[jj * 128:(jj + 1) * 128, t0:t0 + 512], st.ap, [st.r], [R_qmTd], "st_" + st.r.name)
    ts(kmbf.ap, kmacc.ap, 1.0, None, ALU.mult, None, [kmacc.r], [kmbf.r])
    dma("sp", km_d.rearrange("(j p) n -> p j n", p=128), kmbf.ap, [kmbf.r], [R_kmd], "kmst")

    P.barrier(lambda e: e.dma_start(out=bar_d, in_=negh_d))
    dma("sp", lnp.ap, lnrep_d[1], [], [lnp.r], "lnp")
    dma("sp", lncol.ap, lncol_d[1], [], [lncol.r], "lncol")
    dma("sp", pastneg.ap, pastneg_d, [], [pastneg.r], "pastneg")
    dma("sp", trim.ap, trim_d, [], [trim.r], "trim")
    dma("sp", kmT.ap, km_d.rearrange("(h d) n -> d h n", d=64), [R_kmd], [kmT.r], "kmT")
    for sg in range(NSEG):
        dma("sp", KA.ap[64:96, sg * 1024:(sg + 1) * 1024], ohA_d[:, sg * 1024:(sg + 1) * 1024], [], [KR[0][sg]], "K0_%d" % sg)
        dma("sp", KB.ap[64:96, sg * 1024:(sg + 1) * 1024], ohB_d[:, sg * 1024:(sg + 1) * 1024], [], [KR[1][sg]], "K1_%d" % sg)
    for i in range(2):
        P.op("dve", lambda e, i=i: e.memset(biasin[i].ap, 0.0), writes=[biasin[i].r])
    for p in range(ng2):
        t0 = p * 512
        nblk = 2 * p + 2
        nch = 2 * nblk
        dma("sp", xb.ap, x1_d[t0:t0 + 512, :].rearrange("(s p) d -> p s d", p=128), [R_x1d], xbR, "xb")
        for pr in range(2):
            dma("sp", scr(18 + pr), qmT_d[pr * 128:(pr + 1) * 128, t0:t0 + 512], [R_qmTd], [scrR[18 + pr]], "scr%d" % (18 + pr))
        def sel_issue(h):
            qa = Qaug[h % 2]
            bi = biasin[h % 2]
            dma("sp", qa.ap[0:64, :], qT_d[h * 64:(h + 1) * 64, t0:t0 + 512], [R_qTd], [qa.r], "qa%d" % (h % 2))
            for s in range(4):
                mm(psSC.ap[:, s, :], qa.ap[0:64, s * 128:(s + 1) * 128], kmT.ap[:, h, :], True, True, [qa.r, kmT.r], [psSC.r])
            for s in range(4):
                i_blk = 2 * p + s // 2
                tt(scm.ap[:, s, :], psSC.ap[:, s, :], pastneg.ap[:, i_blk, :], ALU.add, [psSC.r, pastneg.r], [scm.r])
            for s in range(4):
                dve(lambda e, s=s: e.max(out=m8.ap[:, s, :], in_=scm.ap[:, s, :]), [scm.r], [m8.r])
            ts(thr.ap, m8.ap[:, :, 2], -2.0e30, None, ALU.max, None, [m8.r], [thr.r])
            for s in range(4):
                ts(bi.ap[:, s, 64:96], scm.ap[:, s, :], thr.ap[:, s:s + 1], 1.0, ALU.is_ge, ALU.subtract, [scm.r, thr.r], [bi.r])

        def sel_finish(h):
            qa = Qaug[h % 2]
            bi = biasin[h % 2]
            for s in range(4):
                tr(psBT.ap[:, s * 128:(s + 1) * 128], bi.ap[:, s, :], [bi.r], [psBT.r])
            act(qa.ap[64:96, :], psBT.ap[64:96, :], AF.Copy, [psBT.r], [qa.r])

        sel_issue(0)
        sel_finish(0)
        for h in range(NH):
            pr, hh = h // 2, h % 2
            qa = Qaug[h % 2]
            for pt_, (Kb_, Vb_, koff, coff) in enumerate(((KA, VA, 0, 0), (KB, VB, 4096, 32))):
                for sg in range((nblk + 3) // 4):
                    b0, b1 = sg * 4, min(sg * 4 + 4, nblk)
                    dma("sp", Kb_.ap[0:64, b0 * 256:b1 * 256], kT_d[h * 64:(h + 1) * 64, koff + b0 * 256:koff + b1 * 256],
                        [R_kTd], [KR[pt_][sg]], "K%d_%d" % (pt_, sg))
                    dma("sp", Vb_.ap[:, 2 * b0:2 * b1, :], v_d[h, :, coff + 2 * b0:coff + 2 * b1, :],
                        [R_vd], [VR[pt_][sg]], "V%d_%d" % (pt_, sg))
            if h + 1 < NH:
                sel_issue(h + 1)
            o = rr("o", psO)
            pairs = [(Kb, Vb, part, blk) for (Kb, Vb, part) in ((KA, VA, 0), (KB, VB, 1)) for blk in range(nblk)]
            total = len(pairs)
            stiles = [(psSS_t[:, :], [psS[0].r, psS[1].r]), (psW.ap, [psW.r])]

            def emit_S(pi):
                Kb, Vb, part, blk = pairs[pi]
                st_ap, st_rs = stiles[pi % 2]
                diag = (part == 0) and (blk >= 2 * p)
                for kc in range(2):
                    cc = blk * 2 + kc
                    ksl = slice(cc * 128, (cc + 1) * 128)
                    so = st_ap[:, kc * 512:(kc + 1) * 512]
                    if not diag:
                        mm(so, Kb.ap[0:96, ksl], qa.ap[0:96, :], True, True, [KR[part][blk // 4], qa.r], st_rs)
                    else:
                        qsel = blk - 2 * p
                        for half in range(2):
                            cs = slice(half * 256, (half + 1) * 256)
                            if half == qsel:
                                mm(so[:, cs], Kb.ap[0:64, ksl], qa.ap[0:64, cs], True, False, [KR[part][blk // 4], qa.r], st_rs)
                                mm(so[:, cs], ident.ap, trim.ap[:, kc, :], False, True, [ident.r, trim.r], st_rs)
                            else:
                                mm(so[:, cs], Kb.ap[0:96, ksl], qa.ap[0:96, cs], True, True, [KR[part][blk // 4], qa.r], st_rs)

            emit_S(0)
            if total > 1:
                emit_S(1)
            for pi in range(total):
                Kb, Vb, part, blk = pairs[pi]
                st_ap, st_rs = stiles[pi % 2]
                pt = PT2[pi % 2]
                act(pt.ap, st_ap, AF.Exp, st_rs, [pt.r])
                if pi + 2 < total:
                    emit_S(pi + 2)
                for kc in range(2):
                    cc = blk * 2 + kc
                    mm(o.ap, Vb.ap[:, cc, :], pt.ap[:, kc * 512:(kc + 1) * 512], pi == 0 and kc == 0, pi == total - 1 and kc == 1,
                       [VR[part][blk // 4], pt.r], [o.r])
            if h + 1 < NH:
                sel_finish(h + 1)
            normalise_into(o, hh, 6 + pr)
        mem_attention(1, None)
        out_proj_ln_ffn(1, True, p)
        dma("sp", out_d[t0:t0 + 512, :].rearrange("(s p) d -> p s d", p=128), xb.ap, xbR, [], "xbst")

    n_inst = P.finalize_and_emit(nc)
    return nc, n_inst


_CACHE = {}


def _consts(r):
    bf = ml_dtypes.bfloat16
    ident = np.eye(128, dtype=np.float32).astype(bf)
    tril = np.tril(np.ones((128, 128), np.float32))
    kpos = np.arange(128)[:, None, None] + 128 * np.arange(2)[None, :, None]
    qpos = np.arange(256)[None, None, :]
    trimask = np.where(kpos > qpos, -NEGV, 0.0).astype(np.float32).astype(bf)
    pn = np.full((16, 32), -3.0e30, np.float32)
    for i in range(16):
        j = 2 * i + r
        for n in range(32):
            jj = 2 * n + r if n < 16 else 2 * (n - 16) + (1 - r)
            if jj < j:
                pn[i, n] = 0.0
    pastneg = np.ascontiguousarray(np.broadcast_to(pn[None], (128, 16, 32)))
    ohA = np.zeros((32, 4096), np.float32)
    ohB = np.zeros((32, 4096), np.float32)
    for n in range(16):
        ohA[n, n * 256:(n + 1) * 256] = NEGV
        ohB[16 + n, n * 256:(n + 1) * 256] = NEGV
    return dict(ident=ident, tril=tril, trimask=trimask, pastneg=pastneg, ohA=ohA.astype(bf), ohB=ohB.astype(bf),
                negh=np.full((128, 1), -0.5, np.float32))


def _in_maps(inp):
    f = lambda a: np.ascontiguousarray(np.asarray(a, dtype=np.float32))
    x = f(inp["x"])
    mem = f(inp["mem"])
    lnrep = np.stack([np.stack([np.broadcast_to(f(inp[k])[L][None, :], (128, D)) for k in ("ln_mix_g", "ln_mix_b", "ln_ffn_g", "ln_ffn_b")], 1)
                      for L in range(2)], 0)
    lnrep = np.ascontiguousarray(lnrep)
    lncol = np.ascontiguousarray(np.stack([np.stack([f(inp[k])[L].reshape(8, 128).T for k in ("ln_mix_g", "ln_mix_b", "ln_ffn_g", "ln_ffn_b")], 1)
                                           for L in range(2)], 0))
    lnv = np.ascontiguousarray(np.stack([np.broadcast_to(f(inp[k])[0][None, :], (128, WMIX)) for k in ("a_ln_v_g", "a_ln_v_b")], 1))
    bs = f(inp["a_b_s"])[0]
    bsT = np.ascontiguousarray(np.repeat(bs.reshape(6, 2, 128).transpose(1, 0, 2), 64, axis=0))
    shared = dict(a_w_in=f(inp["a_w_in"])[0], a_w_s=f(inp["a_w_s"])[0], b_w_in=f(inp["b_w_in"])[0], w_mem_kv=f(inp["w_mem_kv"]),
                  w_mix_out=f(inp["w_mix_out"]), w_ffn_in=f(inp["w_ffn_in"]), w_ffn_out=f(inp["w_ffn_out"]),
                  lnrep=lnrep, lnv=lnv, bsT=bsT, lncol=lncol)
    maps = []
    for c in range(8):
        b, r = c // 2, c % 2
        xb_ = x[b].reshape(16, 2, 256, D)
        xl = np.ascontiguousarray(np.concatenate([xb_[:, r], xb_[:, 1 - r]], 0).reshape(SEQ, D))
        m = dict(shared)
        m.update(_consts(r))
        m["x"] = xl
        m["mem"] = np.ascontiguousarray(mem[b])
        maps.append(m)
    return maps


def kernel(**inputs):
    debug = bool(os.environ.get("MK_DEBUG"))
    key = ("nc", debug)
    if key not in _CACHE:
        _CACHE[key] = build_program(debug, int(os.environ.get("MK_NG1", NG1)), int(os.environ.get("MK_NG2", NG2)), int(os.environ.get("MK_CUT", 99)))
    nc, n_inst = _CACHE[key]
    maps = _in_maps(inputs)
    res = run_bass_kernel_spmd(nc, maps, core_ids=list(range(8)))
    out = np.empty((4, SEQ, D), np.float32)
    for c in range(8):
        b, r = c // 2, c % 2
        o = np.asarray(res.results[c]["out"], dtype=np.float32).reshape(16, 256, D)
        out[b].reshape(16, 2, 256, D)[:, r] = o
    if debug:
        kernel.debug = [np.asarray(res.results[c]["x1dbg"]) for c in range(8)]
    return out
```

```python
import os
import numpy as np
import ml_dtypes
import concourse.bass as bass
import concourse.mybir as mybir
from concourse.bass_utils import run_bass_kernel_spmd

F32 = mybir.dt.float32
BF16 = mybir.dt.bfloat16
AF = mybir.ActivationFunctionType
ALU = mybir.AluOpType
AX = mybir.AxisListType

ENGS = ("pe", "act", "dve", "pool", "sp")

D = 1024
SEQ = 8192
NH = 12
DH = 64
WMIX = 768
NFF = 22
DFF = 2816
ALPHA = 4.0 ** 0.25
LN_EPS = 1e-5
NEGV = 30000.0
NG1 = 16
NG2 = 8


class Res:
    __slots__ = ("name", "last_w", "readers")

    def __init__(self, name):
        self.name = name
        self.last_w = None
        self.readers = []


class Op:
    __slots__ = ("eng", "fn", "deps", "is_dma", "dsem", "token", "signal", "idx", "inc")

    def __init__(self, eng, fn, is_dma=False, dsem=None):
        self.eng = eng
        self.fn = fn
        self.deps = []
        self.is_dma = is_dma
        self.dsem = dsem
        self.token = None
        self.signal = is_dma
        self.idx = None
        self.inc = 16


class Prog:
    def __init__(self):
        self.ops = []
        self.dma_counts = {}
        self.last_dma = {}
        self.last_eng = {}
        self.barrier_op = None

    def op(self, eng, fn, reads=(), writes=(), is_dma=False, dsem=None, inc=16, extra_deps=()):
        o = Op(eng, fn, is_dma, dsem)
        o.inc = inc
        o.idx = len(self.ops)
        deps = {}
        for r in reads:
            w = r.last_w
            if w is not None:
                deps[w.idx] = w
        for r in writes:
            w = r.last_w
            if w is not None and (w.is_dma or is_dma or w.eng != eng):
                if not (w.is_dma and is_dma and w.dsem == dsem):
                    deps[w.idx] = w
            for rd in r.readers:
                if rd.is_dma or is_dma or rd.eng != eng:
                    deps[rd.idx] = rd
        for p in extra_deps:
            deps[p.idx] = p
        if self.barrier_op is not None:
            deps[self.barrier_op.idx] = self.barrier_op
        for r in reads:
            r.readers.append(o)
        for r in writes:
            r.last_w = o
            r.readers = []
        deps.pop(o.idx, None)
        o.deps = list(deps.values())
        for p in o.deps:
            p.signal = True
        if is_dma:
            c = self.dma_counts.get(dsem, 0) + inc
            self.dma_counts[dsem] = c
            o.token = (("dma", dsem), c)
            self.last_dma[dsem] = o
        else:
            self.last_eng[eng] = o
        self.ops.append(o)
        return o

    def barrier(self, fn):
        deps = list(self.last_dma.values()) + list(self.last_eng.values())
        self.barrier_op = None
        b = self.op("sp", fn, is_dma=True, dsem="barrier%d" % len(self.ops), extra_deps=deps)
        self.barrier_op = b
        return b

    def finalize_and_emit(self, nc, final_wait_eng="sp"):
        cnt = {e: 0 for e in ENGS}
        for o in self.ops:
            if not o.is_dma and o.signal:
                cnt[o.eng] += 1
                o.token = (("eng", o.eng), cnt[o.eng])
        sem_keys = [("eng", e) for e in ENGS] + [("dma", k) for k in self.dma_counts]
        sems = {}
        for k in sem_keys:
            sems[k] = nc.alloc_semaphore(name="s_%s_%s" % (k[0], str(k[1])))
        seen = {e: {} for e in ENGS}
        per_eng = {e: [] for e in ENGS}
        for o in self.ops:
            mw = {}
            for p in o.deps:
                key, val = p.token
                if seen[o.eng].get(key, 0) >= val:
                    continue
                seen[o.eng][key] = val
                mw[key] = max(mw.get(key, 0), val)
            per_eng[o.eng].append((o, list(mw.items())))
        finals = [(("dma", k), c) for k, c in self.dma_counts.items()]
        n_inst = {e: 0 for e in ENGS}
        with nc.Block() as block:
            def run(engname, engobj):
                for o, waits in per_eng[engname]:
                    for k, v in waits:
                        engobj.wait_ge(sems[k], v)
                        n_inst[engname] += 1
                    ins = o.fn(engobj)
                    n_inst[engname] += 1
                    if o.is_dma:
                        ins.then_inc(sems[o.token[0]], o.inc)
                    elif o.signal:
                        ins.then_inc(sems[o.token[0]], 1)
                if engname == final_wait_eng:
                    for k, v in finals:
                        if seen[engname].get(k, 0) < v:
                            engobj.wait_ge(sems[k], v)

            @block.tensor
            def _(e):
                run("pe", e)

            @block.scalar
            def _(e):
                run("act", e)

            @block.vector
            def _(e):
                run("dve", e)

            @block.gpsimd
            def _(e):
                run("pool", e)

            @block.sync
            def _(e):
                run("sp", e)
        return n_inst


class Buf:
    __slots__ = ("ap", "r")

    def __init__(self, ap, name):
        self.ap = ap
        self.r = Res(name)


def build_program(debug=False, ng1=NG1, ng2=NG2, cut=99):
    nc = bass.Bass("TRN2", target_bir_lowering=False)
    P = Prog()

    def din(name, shape, dt=F32):
        return nc.dram_tensor(name, list(shape), dt, kind="ExternalInput").ap()

    x_d = din("x", [SEQ, D])
    mem_d = din("mem", [256, D])
    a_w_in = din("a_w_in", [D, 1792])
    a_w_s = din("a_w_s", [12, 128, 128])
    b_w_in = din("b_w_in", [D, 2560])
    w_mem_kv = din("w_mem_kv", [2, D, 512])
    w_mix_out = din("w_mix_out", [2, D, D])
    w_ffn_in = din("w_ffn_in", [2, D, 2 * DFF])
    w_ffn_out = din("w_ffn_out", [2, DFF, D])
    lnrep_d = din("lnrep", [2, 128, 4, D])
    lnv_d = din("lnv", [128, 2, WMIX])
    lncol_d = din("lncol", [2, 128, 4, 8])
    bsT_d = din("bsT", [128, 6, 128])
    negh_d = din("negh", [128, 1])
    ident_d = din("ident", [128, 128], BF16)
    tril_d = din("tril", [128, 128])
    trim_d = din("trimask", [128, 2, 256], BF16)
    pastneg_d = din("pastneg", [128, 16, 32])
    ohA_d = din("ohA", [32, 4096], BF16)
    ohB_d = din("ohB", [32, 4096], BF16)
    out_d = nc.dram_tensor("out", [4096, D], F32, kind="ExternalOutput").ap()
    def dscr(name, shape, dt):
        return nc.dram_tensor(name, list(shape), dt, kind="ExternalOutput").ap()

    x1_d = dscr("x1dbg", [4096, D], F32)
    qT_d = dscr("qTs", [768, 4096], BF16)
    qmT_d = dscr("qmTs", [256, 4096], BF16)
    kT_d = dscr("kTs", [768, SEQ], BF16)
    v_d = dscr("vs", [12, 128, 64, 128], BF16)
    km_d = dscr("kms", [768, 32], BF16)
    bar_d = nc.dram_tensor("bars", [128, 1], F32).ap()
    R_x1d, R_qTd, R_qmTd, R_kTd, R_vd, R_kmd = (Res(n) for n in ("x1d", "qTd", "qmTd", "kTd", "vd", "kmd"))

    def sb(name, shape, dt):
        return Buf(nc.alloc_sbuf_tensor("sb_" + name, list(shape), dt)[:], name)

    def ps(name, shape, dt=F32):
        return Buf(nc.alloc_psum_tensor(name, list(shape), dt)[:], name)

    xb = sb("xb", [128, 4, D], F32)
    xbR = [Res("xb%d" % i) for i in range(4)]
    xT = sb("xT", [128, 8, 512], BF16)
    xT2 = sb("xT2", [128, 8, 512], BF16)
    scr_t = nc.alloc_sbuf_tensor("sb_scr", [128, NFF, 512], BF16)
    scrR = [Res("scr%d" % i) for i in range(NFF)]
    wout_t = nc.alloc_sbuf_tensor("sb_wout", [128, NFF, D], BF16)
    woutR = [Res("wout%d" % i) for i in range(11)]
    wsl = [sb("wsl%d" % i, [128, 8, 512], BF16) for i in range(3)]
    lnp = sb("lnp", [128, 4, D], F32)
    lncol = sb("lncol", [128, 4, 8], F32)
    ident = sb("ident", [128, 128], BF16)
    kmemT = sb("kmemT", [128, 4, 256], BF16)
    vmem = sb("vmem", [128, 16, 128], BF16)
    tmpbf = [sb("tmpbf%d" % i, [128, D], BF16) for i in range(4)]
    tbq = tmpbf
    tA = [sb("tA%d" % i, [128, D], F32) for i in range(2)]
    PT2 = [sb("PT2_%d" % i, [128, 1024], BF16) for i in range(2)]
    PT = PT2
    negh = sb("negh", [128, 1], F32)
    epsb = sb("epsb", [128, 1], F32)
    bnst = [sb("bnst%d" % i, [128, 2, 6], F32) for i in range(4)]
    mv = [sb("mv%d" % i, [128, 2], F32) for i in range(4)]
    rstd = [sb("rstd%d" % i, [128, 1], F32) for i in range(4)]
    nbias = [sb("nbias%d" % i, [128, 1], F32) for i in range(4)]
    UNI = 40 * 1024
    uni_t = nc.alloc_sbuf_tensor("sb_uni", [128, UNI // 2], BF16)

    class Carver:
        def __init__(self):
            self.off = 0

        def take(self, name, shape, dt, parts=128):
            n = int(np.prod(shape[1:]))
            nb = n * (4 if dt == F32 else 2)
            nb_al = (nb + 63) // 64 * 64
            a = uni_t[0:parts, self.off // 2:(self.off + nb) // 2]
            if dt == F32:
                a = a.bitcast(F32)
            if len(shape) == 3:
                a = a.rearrange("p (a b) -> p a b", b=shape[2])
            elif len(shape) == 4:
                a = a.rearrange("p (a b c) -> p a b c", b=shape[2], c=shape[3])
            self.off += nb_al
            assert self.off <= UNI, (name, self.off)
            return Buf(a, name)

    c1 = Carver()
    vst = [c1.take("vst%d" % i, [128, 12, 128], BF16) for i in range(2)]
    stg = [c1.take("stg%d" % i, [128, 512], BF16) for i in range(3)]
    lnv = c1.take("lnv", [128, 2, WMIX], F32)
    bsT = c1.take("bsT", [128, 6, 128], F32)
    WsT = c1.take("WsT", [128, 12, 128], BF16)
    kmacc = c1.take("kmacc", [128, 6, 32], F32)
    kmbf = c1.take("kmbf", [128, 6, 32], BF16)
    wsl_extra = [c1.take("wslx%d" % i, [128, 8, 512], BF16) for i in range(2)]
    wsl_cur = [wsl + wsl_extra]
    c2 = Carver()
    KA = c2.take("KA", [96, 4096], BF16, parts=96)
    KB = c2.take("KB", [96, 4096], BF16, parts=96)
    VA = c2.take("VA", [128, 32, 128], BF16)
    VB = c2.take("VB", [128, 32, 128], BF16)
    Qaug = [c2.take("Qaug%d" % i, [96, 512], BF16, parts=96) for i in range(2)]
    pastneg = c2.take("pastneg", [128, 16, 32], F32)
    trim = c2.take("trim", [128, 2, 256], BF16)
    kmT = c2.take("kmT", [64, 12, 32], BF16, parts=64)
    scm = c2.take("scm", [128, 4, 32], F32)
    m8 = c2.take("m8", [128, 4, 8], F32)
    thr = c2.take("thr", [128, 4], F32)
    biasin = [c2.take("biasin%d" % i, [128, 4, 96], BF16) for i in range(2)]
    NSEG = 4
    KR = [[Res("K%d_%d" % (pt_, sg)) for sg in range(NSEG)] for pt_ in range(2)]
    VR = [[Res("V%d_%d" % (pt_, sg)) for sg in range(NSEG)] for pt_ in range(2)]

    psSS_t = nc.alloc_psum_tensor("psSS", [128, 1024], F32)
    psS = [Buf(psSS_t[:, 512 * i:512 * (i + 1)], "psS%d" % i) for i in range(2)]
    psO = [ps("psO%d" % i, [128, 512]) for i in range(2)]
    psW = ps("psW", [128, 1024])
    psM_t = nc.alloc_psum_tensor("psM", [128, 1024], BF16)
    psSC = Buf(psM_t[:, 0:256].bitcast(F32).rearrange("p (a b) -> p a b", b=32), "psSC")
    psBT = Buf(psM_t[0:96, 512:1024], "psBT")
    psBT.r = psSC.r
    psT = ps("psT", [128, 1024], BF16)

    cnt = {"s": 0, "o": 0, "w": 0, "pt": 0, "stg": 0, "tb": 0, "ta": 0, "st": 0}

    def rr(key, lst):
        i = cnt[key]
        cnt[key] = i + 1
        return lst[i % len(lst)]

    def dma(q, out, in_, reads, writes, dsem):
        return P.op(q, lambda e: e.dma_start(out=out, in_=in_), reads=reads, writes=writes, is_dma=True, dsem=dsem)

    def mm(out, lhsT, rhs, start, stop, reads, writes, tile_position=None):
        if tile_position is None:
            return P.op("pe", lambda e: e.matmul(out, lhsT=lhsT, rhs=rhs, start=start, stop=stop), reads=reads, writes=writes)
        return P.op("pe", lambda e: e.matmul(out, lhsT=lhsT, rhs=rhs, start=start, stop=stop, tile_position=tile_position),
                    reads=reads, writes=writes)

    def tr(out, in_, reads, writes):
        n = in_.shape[0]
        return P.op("pe", lambda e: e.transpose(out=out, in_=in_, identity=ident.ap[0:n, 0:n]), reads=list(reads) + [ident.r], writes=writes)

    def act(out, in_, func, reads, writes, scale=1.0, bias=0.0):
        return P.op("act", lambda e: e.activation(out=out, in_=in_, func=func, bias=bias, scale=scale), reads=reads, writes=writes)

    def dve(fn, reads, writes):
        return P.op("dve", fn, reads=reads, writes=writes)

    def tt(out, in0, in1, op, reads, writes, eng="dve"):
        return P.op(eng, lambda e: e.tensor_tensor(out=out, in0=in0, in1=in1, op=op), reads=reads, writes=writes)

    def ts(out, in0, s1, s2, op0, op1, reads, writes):
        if s2 is None:
            return P.op("dve", lambda e: e.tensor_scalar(out=out, in0=in0, scalar1=s1, scalar2=None, op0=op0), reads=reads, writes=writes)
        return P.op("dve", lambda e: e.tensor_scalar(out=out, in0=in0, scalar1=s1, scalar2=s2, op0=op0, op1=op1), reads=reads, writes=writes)

    wcnt = [0]

    def wload(src3, width):
        s = wsl_cur[0][wcnt[0] % len(wsl_cur[0])]
        wcnt[0] += 1
        dma("pool", s.ap[:, :, 0:width], src3, [], [s.r], "wsl_" + s.r.name)
        return s

    def wpiece(w2d, c0, width):
        return w2d.rearrange("(kc p) c -> p kc c", p=128)[:, :, c0:c0 + width]

    def scr(i):
        return scr_t[:, i, :]

    dma("sp", ident.ap, ident_d, [], [ident.r], "ident")
    dma("sp", negh.ap, negh_d, [], [negh.r], "negh")
    P.op("dve", lambda e: e.memset(epsb.ap, LN_EPS), writes=[epsb.r])
    dma("sp", lnv.ap, lnv_d, [], [lnv.r], "lnv")
    dma("sp", bsT.ap, bsT_d, [], [bsT.r], "bsT")
    dma("sp", lnp.ap, lnrep_d[0], [], [lnp.r], "lnp")
    dma("sp", lncol.ap, lncol_d[0], [], [lncol.r], "lncol")
    wsf = tA[0]
    trl = tA[1]
    wsm = tmpbf[0]
    dma("sp", trl.ap[:, 0:128], tril_d, [], [trl.r], "tA1")
    for half in range(2):
        dma("sp", wsf.ap[:, 0:768].rearrange("p (g s) -> p g s", s=128), a_w_s[6 * half:6 * half + 6].rearrange("g t s -> t g s"),
            [], [wsf.r], "tA0")
        for gl in range(6):
            tt(wsm.ap[:, gl * 128:(gl + 1) * 128], wsf.ap[:, gl * 128:(gl + 1) * 128], trl.ap[:, 0:128], ALU.mult,
               [wsf.r, trl.r], [wsm.r])
        for gl in range(6):
            tr(psT.ap[:, gl * 128:(gl + 1) * 128], wsm.ap[:, gl * 128:(gl + 1) * 128], [wsm.r], [psT.r])
        dve(lambda e, half=half: e.tensor_copy(out=WsT.ap[:, half * 6:(half + 1) * 6, :],
                                               in_=psT.ap[:, 0:768].rearrange("p (g t) -> p g t", t=128)),
            [psT.r], [WsT.r])

    memT = xT2
    for mc in range(2):
        dma("sp", tA[mc].ap, mem_d[mc * 128:(mc + 1) * 128, :], [], [tA[mc].r], "tA%d" % mc)
        act(tmpbf[mc].ap, tA[mc].ap, AF.Copy, [tA[mc].r], [tmpbf[mc].r])
        for kc in range(8):
            tr(psT.ap[:, kc * 128:(kc + 1) * 128], tmpbf[mc].ap[:, kc * 128:(kc + 1) * 128], [tmpbf[mc].r], [psT.r])
        dve(lambda e, mc=mc: e.tensor_copy(out=memT.ap[:, :, mc * 128:(mc + 1) * 128],
                                           in_=psT.ap.rearrange("p (k t) -> p k t", t=128)), [psT.r], [memT.r])
    P.op("dve", lambda e: e.memset(vmem.ap, 1.0), writes=[vmem.r])
    for L in range(2):
        w = wload(wpiece(w_mem_kv[L], 0, 512), 512)
        for pr in range(2):
            o = rr("s", psS)
            for kc in range(8):
                mm(o.ap[:, 0:256], w.ap[:, kc, pr * 128:(pr + 1) * 128], memT.ap[:, kc, 0:256], kc == 0, kc == 7, [w.r, memT.r], [o.r])
            act(kmemT.ap[:, L * 2 + pr, :], o.ap[:, 0:256], AF.Copy, [o.r], [kmemT.r])
        for mc in range(2):
            o = rr("s", psS)
            for kc in range(8):
                mm(o.ap[:, 0:256], memT.ap[:, kc, mc * 128:(mc + 1) * 128], w.ap[:, kc, 256:512], kc == 0, kc == 7, [w.r, memT.r], [o.r])
            ov = o.ap[:, 0:256].rearrange("p (pr hh d) -> p pr hh d", hh=2, d=64)
            vv = vmem.ap[:, (L * 2 + mc) * 4:(L * 2 + mc) * 4 + 4, :].rearrange("p (pr hh) c -> p pr hh c", hh=2)
            act(vv[:, :, 0, 0:64], ov[:, :, 0, :], AF.Copy, [o.r], [vmem.r])
            act(vv[:, :, 1, 64:128], ov[:, :, 1, :], AF.Copy, [o.r], [vmem.r])

    def transposes_to(dst, s, srcbuf, vi=None):
        for kc in range(8):
            tr(psT.ap[:, kc * 128:(kc + 1) * 128], srcbuf.ap[:, kc * 128:(kc + 1) * 128], [srcbuf.r], [psT.r])
        if vi is None:
            act(dst.ap[:, :, s * 128:(s + 1) * 128], psT.ap.rearrange("p (k t) -> p k t", t=128), AF.Copy, [psT.r], [dst.r])
        else:
            for kc in range(8):
                P.op("act", lambda e, kc=kc: e.activation(out=dst.ap[:, kc, s * 128:(s + 1) * 128], in_=psT.ap[:, kc * 128:(kc + 1) * 128],
                                                         func=AF.Identity, bias=lncol.ap[:, vi + 1, kc:kc + 1], scale=lncol.ap[:, vi, kc:kc + 1]),
                     reads=[psT.r, lncol.r], writes=[dst.r])

    def ln_stats(buf_ap, width, bufR, k):
        nchunk = 2
        cw = width // nchunk
        for c in range(nchunk):
            dve(lambda e, c=c: e.bn_stats(out=bnst[k].ap[:, c, :], in_=buf_ap[:, c * cw:(c + 1) * cw]), [bufR], [bnst[k].r])
        dve(lambda e: e.bn_aggr(out=mv[k].ap, in_=bnst[k].ap.rearrange("p a b -> p (a b)")), [bnst[k].r], [mv[k].r])
        act(nbias[k].ap, mv[k].ap[:, 1:2], AF.Sqrt, [mv[k].r, epsb.r], [nbias[k].r], bias=epsb.ap)
        dve(lambda e: e.reciprocal(out=rstd[k].ap, in_=nbias[k].ap), [nbias[k].r], [rstd[k].r])
        ts(nbias[k].ap, mv[k].ap[:, 0:1], -1.0, rstd[k].ap, ALU.mult, ALU.mult, [mv[k].r, rstd[k].r], [nbias[k].r])

    def layer_norm_v(buf_ap, width, bufR, gam, bet, gbR, outbf_ap, outbfRs, k):
        ln_stats(buf_ap, width, bufR, k)
        P.op("act", lambda e: e.activation(out=buf_ap, in_=buf_ap, func=AF.Identity, bias=nbias[k].ap, scale=rstd[k].ap),
             reads=[bufR, nbias[k].r, rstd[k].r], writes=[bufR])
        tt(buf_ap, buf_ap, gam, ALU.mult, [bufR, gbR], [bufR])
        tt(outbf_ap, buf_ap, bet, ALU.add, [bufR, gbR], list(outbfRs))

    def layer_norm_res(buf_ap, bufR, gi, tb, k):
        ln_stats(buf_ap, D, bufR, k)
        if tb is not None:
            P.op("act", lambda e: e.activation(out=tb.ap, in_=buf_ap, func=AF.Identity, bias=nbias[k].ap, scale=rstd[k].ap),
                 reads=[bufR, nbias[k].r, rstd[k].r], writes=[tb.r])

    def layer_norm_res_finish(buf_ap, bufR, gi, k):
        ts(buf_ap, buf_ap, mv[k].ap[:, 0:1], rstd[k].ap, ALU.subtract, ALU.mult, [bufR, mv[k].r, rstd[k].r], [bufR])
        tt(buf_ap, buf_ap, lnp.ap[:, gi, :], ALU.mult, [bufR, lnp.r], [bufR])
        tt(buf_ap, buf_ap, lnp.ap[:, gi + 1, :], ALU.add, [bufR, lnp.r], [bufR])

    def mem_attention(L, qmR):
        for h in range(4):
            pr, hh = h // 2, h % 2
            pb = 64 * hh
            o = rr("o", psO)
            for mc in range(2):
                s_ = rr("s", psS)
                mm(s_.ap, kmemT.ap[pb:pb + 64, L * 2 + pr, mc * 128:(mc + 1) * 128], scr_t[pb:pb + 64, 18 + pr, :], True, True,
                   [kmemT.r, scrR[18 + pr]], [s_.r])
                pt = rr("pt", PT)
                act(pt.ap[:, 0:512], s_.ap, AF.Exp, [s_.r], [pt.r])
                mm(o.ap, vmem.ap[:, (L * 2 + mc) * 4 + h, :], pt.ap[:, 0:512], mc == 0, mc == 1, [vmem.r, pt.r], [o.r])
            normalise_into(o, hh, 20 + pr)

    def normalise_into(o, hh, scr_idx):
        ob, db = (0, 64) if hh == 0 else (64, 0)
        rec = rr("ta", tA)
        dve(lambda e: e.reciprocal(out=rec.ap[db:db + 64, 0:512], in_=o.ap[db:db + 64, :]), [o.r], [rec.r])
        tt(scr_t[ob:ob + 64, scr_idx, :], o.ap[ob:ob + 64, :], rec.ap[db:db + 64, 0:512], ALU.mult, [o.r, rec.r], [scrR[scr_idx]])

    def out_proj_ln_ffn(L, last, G, hook=None):
        wo = [wload(wpiece(w_mix_out[L], hf * 512, 512), 512) for hf in range(2)]
        for j in range(11):
            dma("pool", wout_t[:, 2 * j:2 * j + 2, :], w_ffn_out[L, 256 * j:256 * (j + 1), :].rearrange("(f p) c -> p f c", p=128),
                [], [woutR[j]], "wout%d" % j)
        for s in range(4):
            for hf in range(2):
                y = rr("o", psO)
                for c in range(8):
                    si = 6 + c if c < 6 else 20 + (c - 6)
                    mm(y.ap, scr_t[:, si, s * 128:(s + 1) * 128], wo[hf].ap[:, c, :], c == 0, c == 7, [scrR[si], wo[hf].r], [y.r])
                xs = xb.ap[:, s, hf * 512:(hf + 1) * 512]
                dve(lambda e, xs=xs, y=y: e.scalar_tensor_tensor(out=xs, in0=xs, scalar=ALPHA, in1=y.ap, op0=ALU.mult, op1=ALU.add),
                    [xbR[s], y.r], [xbR[s]])
            layer_norm_res(xb.ap[:, s, :], xbR[s], 0, tbq[s], s)
        if hook is not None:
            hook()
        for s in range(4):
            transposes_to(xT2, s, tbq[s], vi=0)
        for s in range(4):
            layer_norm_res_finish(xb.ap[:, s, :], xbR[s], 0, s)
        for j in range(11):
            w = wsl_cur[0][wcnt[0] % len(wsl_cur[0])]
            wcnt[0] += 1
            for two in range(2):
                dma("pool", w.ap[:, :, 256 * two:256 * (two + 1)], wpiece(w_ffn_in[L], two * DFF + 256 * j, 256), [], [w.r], "wsl_" + w.r.name)
            for fl in range(2):
                f = 2 * j + fl
                pg = rr("s", psS)
                pu = rr("o", psO)
                for kc in range(8):
                    mm(pg.ap, w.ap[:, kc, fl * 128:(fl + 1) * 128], xT2.ap[:, kc, :], kc == 0, kc == 7, [w.r, xT2.r], [pg.r])
                for kc in range(8):
                    mm(pu.ap, w.ap[:, kc, 256 + fl * 128:256 + (fl + 1) * 128], xT2.ap[:, kc, :], kc == 0, kc == 7, [w.r, xT2.r], [pu.r])
                sg = rr("ta", tA)
                act(sg.ap[:, 0:512], pg.ap, AF.Silu, [pg.r], [sg.r])
                tt(scr(f), sg.ap[:, 0:512], pu.ap, ALU.mult, [sg.r, pu.r], [scrR[f]])
        for s in range(4):
            for hf in range(2):
                y = rr("o", psO)
                for f in range(NFF):
                    mm(y.ap, scr_t[:, f, s * 128:(s + 1) * 128], wout_t[:, f, hf * 512:(hf + 1) * 512], f == 0, f == NFF - 1,
                       [scrR[f], woutR[f // 2]], [y.r])
                xs = xb.ap[:, s, hf * 512:(hf + 1) * 512]
                dve(lambda e, xs=xs, y=y: e.scalar_tensor_tensor(out=xs, in0=xs, scalar=ALPHA, in1=y.ap, op0=ALU.mult, op1=ALU.add),
                    [xbR[s], y.r], [xbR[s]])
            layer_norm_res(xb.ap[:, s, :], xbR[s], 2, None if last else tbq[s], s)
        if not last:
            for s in range(4):
                transposes_to(xT, s, tbq[s], vi=2)
        for s in range(4):
            layer_norm_res_finish(xb.ap[:, s, :], xbR[s], 2, s)

    P.op("dve", lambda e: e.memset(kmacc.ap, 0.0), writes=[kmacc.r])
    for i in range(2):
        P.op("dve", lambda e, i=i: e.memset(vst[i].ap, 1.0), writes=[vst[i].r])
    for G in range(ng1):
        own = G < NG2
        t0 = G * 512
        dma("sp", xb.ap, x_d[t0:t0 + 512, :].rearrange("(s p) d -> p s d", p=128), [], xbR, "xb")
        for s in range(4):
            tb = rr("tb", tmpbf)
            act(tb.ap, xb.ap[:, s, :], AF.Copy, [xbR[s]], [tb.r])
            transposes_to(xT, s, tb)
        if cut <= 1:
            continue
        for (c0, wd, base) in ((0, 512, 0), (512, 256, 4)):
            w = wload(wpiece(a_w_in, c0, wd), wd)
            for jj in range(wd // 128):
                o = rr("s", psS)
                for kc in range(8):
                    mm(o.ap, w.ap[:, kc, jj * 128:(jj + 1) * 128], xT.ap[:, kc, :], kc == 0, kc == 7, [w.r, xT.r], [o.r])
                act(scr(base + jj), o.ap, AF.Gelu_apprx_tanh, [o.r], [scrR[base + jj]])
        w = wload(wpiece(a_w_in, 1536, 256), 256)
        for jj in range(2):
            o = rr("s", psS)
            for kc in range(8):
                mm(o.ap, w.ap[:, kc, jj * 128:(jj + 1) * 128], xT.ap[:, kc, :], kc == 0, kc == 7, [w.r, xT.r], [o.r])
            act(scr(18 + jj), o.ap, AF.Copy, [o.r], [scrR[18 + jj]], scale=0.125)
        if cut <= 2:
            continue
        wv0 = wload(wpiece(a_w_in, 768, 512), 512)
        wv1 = wload(wpiece(a_w_in, 1280, 256), 256)
        vR = scrR[12:18]
        v_all = scr_t[:, 12:18, :].rearrange("p a b -> p (a b)").rearrange("p (s c) -> p s c", c=WMIX)
        for s in range(4):
            for kc in range(8):
                mm(psW.ap[:, 0:512], xT.ap[:, kc, s * 128:(s + 1) * 128], wv0.ap[:, kc, :], kc == 0, kc == 7, [xT.r, wv0.r], [psW.r])
            for kc in range(8):
                mm(psW.ap[:, 512:768], xT.ap[:, kc, s * 128:(s + 1) * 128], wv1.ap[:, kc, 0:256], kc == 0, kc == 7, [xT.r, wv1.r], [psW.r])
            vg = rr("ta", tA)
            act(vg.ap[:, 0:512], psW.ap[:, 0:512], AF.Gelu_apprx_tanh, [psW.r], [vg.r])
            act(vg.ap[:, 512:768], psW.ap[:, 512:768], AF.Gelu_apprx_tanh, [psW.r], [vg.r])
            layer_norm_v(vg.ap[:, 0:768], WMIX, vg.r, lnv.ap[:, 0, :], lnv.ap[:, 1, :], lnv.r, v_all[:, s, :], vR, s)
        if cut <= 3:
            continue
        for s in range(4):
            for g in range(12):
                mm(psW.ap[(g % 2) * 64:(g % 2) * 64 + 64, (g // 2) * 128:(g // 2 + 1) * 128], v_all[:, s, g * 64:(g + 1) * 64], WsT.ap[:, g, :],
                   True, True, vR + [WsT.r], [psW.r], tile_position=((0, 64) if g % 2 else None))
            sv = rr("ta", tA)
            tt(sv.ap[:, 0:768], psW.ap[:, 0:768], bsT.ap.rearrange("p a b -> p (a b)"), ALU.add, [psW.r, bsT.r], [sv.r])
            tt(scr_t[:, 6:12, s * 128:(s + 1) * 128], sv.ap[:, 0:768].rearrange("p (a b) -> p a b", b=128), scr_t[:, 0:6, s * 128:(s + 1) * 128],
               ALU.mult, [sv.r] + scrR[0:6], scrR[6:12])
        if cut <= 4:
            continue
        mem_attention(0, None)
        if cut <= 5:
            continue
        out_proj_ln_ffn(0, False, G)
        if own:
            dma("sp", x1_d[t0:t0 + 512, :].rearrange("(s p) d -> p s d", p=128), xb.ap, xbR, [R_x1d], "xbst")
        if cut <= 6:
            continue
        for (c0, wd, jbase) in ((768, 512, 0), (1280, 256, 4)):
            w = wload(wpiece(b_w_in, c0, wd), wd)
            for jj in range(wd // 128):
                j = jbase + jj
                o = rr("s", psS)
                for kc in range(8):
                    mm(o.ap, w.ap[:, kc, jj * 128:(jj + 1) * 128], xT.ap[:, kc, :], kc == 0, kc == 7, [w.r, xT.r], [o.r])
                st = rr("stg", stg)
                act(st.ap, o.ap, AF.Copy, [o.r], [st.r])
                for b2 in range(2):
                    dve(lambda e, b2=b2, st=st: e.bn_stats(out=bnst[0].ap[:, b2, :], in_=st.ap[:, b2 * 256:(b2 + 1) * 256]), [st.r], [bnst[0].r])
                    dve(lambda e, b2=b2: e.bn_aggr(out=mv[0].ap, in_=bnst[0].ap[:, b2, :]), [bnst[0].r], [mv[0].r])
                    dve(lambda e, b2=b2, j=j, G=G: e.tensor_copy(out=kmacc.ap[:, j, 2 * G + b2:2 * G + b2 + 1], in_=mv[0].ap[:, 0:1]),
                        [mv[0].r], [kmacc.r])
                dma("sp", kT_d[j * 128:(j + 1) * 128, t0:t0 + 512], st.ap, [st.r], [R_kTd], "st_" + st.r.name)
        if cut <= 7:
            continue
        wv0 = wload(wpiece(b_w_in, 1536, 512), 512)
        wv1 = wload(wpiece(b_w_in, 2048, 256), 256)
        for s in range(4):
            for kc in range(8):
                mm(psW.ap[:, 0:512], xT.ap[:, kc, s * 128:(s + 1) * 128], wv0.ap[:, kc, :], kc == 0, kc == 7, [xT.r, wv0.r], [psW.r])
            for kc in range(8):
                mm(psW.ap[:, 512:768], xT.ap[:, kc, s * 128:(s + 1) * 128], wv1.ap[:, kc, 0:256], kc == 0, kc == 7, [xT.r, wv1.r], [psW.r])
            vs_ = rr("st", vst)
            for (lo, hi, p0, p1) in ((0, 512, 0, 4), (512, 768, 4, 6)):
                pv = psW.ap[:, lo:hi].rearrange("p (pr hh d) -> p pr hh d", hh=2, d=64)
                vv = vs_.ap[:, 2 * p0:2 * p1, :].rearrange("p (pr hh) c -> p pr hh c", hh=2)
                act(vv[:, :, 0, 0:64], pv[:, :, 0, :], AF.Copy, [psW.r], [vs_.r])
                act(vv[:, :, 1, 64:128], pv[:, :, 1, :], AF.Copy, [psW.r], [vs_.r])
            for q4 in range(4):
                dma("sp", v_d[3 * q4:3 * q4 + 3, :, G * 4 + s, :].rearrange("h p c -> p h c"), vs_.ap[:, 3 * q4:3 * q4 + 3, :],
                    [vs_.r], [R_vd], "st_" + vs_.r.name)
        if own:
            for (c0, wd, jbase) in ((0, 512, 0), (512, 256, 4)):
                w = wload(wpiece(b_w_in, c0, wd), wd)
                for jj in range(wd // 128):
                    j = jbase + jj
                    o = rr("s", psS)
                    for kc in range(8):
                        mm(o.ap, w.ap[:, kc, jj * 128:(jj + 1) * 128], xT.ap[:, kc, :], kc == 0, kc == 7, [w.r, xT.r], [o.r])
                    st = rr("stg", stg)
                    act(st.ap, o.ap, AF.Copy, [o.r], [st.r], scale=0.125)
                    dma("sp", qT_d[j * 128:(j + 1) * 128, t0:t0 + 512], st.ap, [st.r], [R_qTd], "st_" + st.r.name)
            w = wload(wpiece(b_w_in, 2304, 256), 256)
            for jj in range(2):
                o = rr("s", psS)
                for kc in range(8):
                    mm(o.ap, w.ap[:, kc, jj * 128:(jj + 1) * 128], xT.ap[:, kc, :], kc == 0, kc == 7, [w.r, xT.r], [o.r])
                st = rr("stg", stg)
                act(st.ap, o.ap, AF.Copy, [o.r], [st.r], scale=0.125)
                dma("sp", qmT_d[jj * 128:(jj + 1) * 128, t0:t0 + 512], st.ap, [st.r], [R_qmTd], "st_" + st.r.name)
    ts(kmbf.ap, kmacc.ap, 1.0, None, ALU.mult, None, [kmacc.r], [kmbf.r])
    dma("sp", km_d.rearrange("(j p) n -> p j n", p=128), kmbf.ap, [kmbf.r], [R_kmd], "kmst")

    P.barrier(lambda e: e.dma_start(out=bar_d, in_=negh_d))
    wsl_cur[0] = wsl
    dma("sp", lnp.ap, lnrep_d[1], [], [lnp.r], "lnp")
    dma("sp", lncol.ap, lncol_d[1], [], [lncol.r], "lncol")
    dma("sp", pastneg.ap, pastneg_d, [], [pastneg.r], "pastneg")
    dma("sp", trim.ap, trim_d, [], [trim.r], "trim")
    dma("sp", kmT.ap, km_d.rearrange("(h d) n -> d h n", d=64), [R_kmd], [kmT.r], "kmT")
    for sg in range(NSEG):
        dma("sp", KA.ap[64:96, sg * 1024:(sg + 1) * 1024], ohA_d[:, sg * 1024:(sg + 1) * 1024], [], [KR[0][sg]], "K0_%d" % sg)
        dma("sp", KB.ap[64:96, sg * 1024:(sg + 1) * 1024], ohB_d[:, sg * 1024:(sg + 1) * 1024], [], [KR[1][sg]], "K1_%d" % sg)
    for i in range(2):
        P.op("dve", lambda e, i=i: e.memset(biasin[i].ap, 0.0), writes=[biasin[i].r])
    for p in range(ng2):
        t0 = p * 512
        nblk = 2 * p + 2
        nch = 2 * nblk
        dma("sp", xb.ap, x1_d[t0:t0 + 512, :].rearrange("(s p) d -> p s d", p=128), [R_x1d], xbR, "xb")
        for pr in range(2):
            dma("sp", scr(18 + pr), qmT_d[pr * 128:(pr + 1) * 128, t0:t0 + 512], [R_qmTd], [scrR[18 + pr]], "scr%d" % (18 + pr))
        def sel_issue(h, p=p, t0=t0):
            qa = Qaug[h % 2]
            bi = biasin[h % 2]
            dma("sp", qa.ap[0:64, :], qT_d[h * 64:(h + 1) * 64, t0:t0 + 512], [R_qTd], [qa.r], "qa%d" % (h % 2))
            for s in range(4):
                mm(psSC.ap[:, s, :], qa.ap[0:64, s * 128:(s + 1) * 128], kmT.ap[:, h, :], True, True, [qa.r, kmT.r], [psSC.r])
            for s in range(4):
                i_blk = 2 * p + s // 2
                tt(scm.ap[:, s, :], psSC.ap[:, s, :], pastneg.ap[:, i_blk, :], ALU.add, [psSC.r, pastneg.r], [scm.r])
            for s in range(4):
                dve(lambda e, s=s: e.max(out=m8.ap[:, s, :], in_=scm.ap[:, s, :]), [scm.r], [m8.r])
            ts(thr.ap, m8.ap[:, :, 2], -2.0e30, None, ALU.max, None, [m8.r], [thr.r])
            for s in range(4):
                ts(bi.ap[:, s, 64:96], scm.ap[:, s, :], thr.ap[:, s:s + 1], 1.0, ALU.is_ge, ALU.subtract, [scm.r, thr.r], [bi.r])

        def sel_finish(h, p=p, t0=t0):
            qa = Qaug[h % 2]
            bi = biasin[h % 2]
            for s in range(4):
                tr(psBT.ap[:, s * 128:(s + 1) * 128], bi.ap[:, s, :], [bi.r], [psBT.r])
            act(qa.ap[64:96, :], psBT.ap[64:96, :], AF.Copy, [psBT.r], [qa.r])

        if p == 0:
            sel_issue(0)
            sel_finish(0)
        for h in range(NH):
            pr, hh = h // 2, h % 2
            qa = Qaug[h % 2]
            for pt_, (Kb_, Vb_, koff, coff) in enumerate(((KA, VA, 0, 0), (KB, VB, 4096, 32))):
                for sg in range((nblk + 3) // 4):
                    b0, b1 = sg * 4, min(sg * 4 + 4, nblk)
                    dma("sp", Kb_.ap[0:64, b0 * 256:b1 * 256], kT_d[h * 64:(h + 1) * 64, koff + b0 * 256:koff + b1 * 256],
                        [R_kTd], [KR[pt_][sg]], "K%d_%d" % (pt_, sg))
                    dma("sp", Vb_.ap[:, 2 * b0:2 * b1, :], v_d[h, :, coff + 2 * b0:coff + 2 * b1, :],
                        [R_vd], [VR[pt_][sg]], "V%d_%d" % (pt_, sg))
            if h + 1 < NH:
                sel_issue(h + 1)
            o = rr("o", psO)
            pairs = [(Kb, Vb, part, blk) for (Kb, Vb, part) in ((KA, VA, 0), (KB, VB, 1)) for blk in range(nblk)]
            total = len(pairs)
            stiles = [(psSS_t[:, :], [psS[0].r, psS[1].r]), (psW.ap, [psW.r])]

            def emit_S(pi):
                Kb, Vb, part, blk = pairs[pi]
                st_ap, st_rs = stiles[pi % 2]
                diag = (part == 0) and (blk >= 2 * p)
                for kc in range(2):
                    cc = blk * 2 + kc
                    ksl = slice(cc * 128, (cc + 1) * 128)
                    so = st_ap[:, kc * 512:(kc + 1) * 512]
                    if not diag:
                        mm(so, Kb.ap[0:96, ksl], qa.ap[0:96, :], True, True, [KR[part][blk // 4], qa.r], st_rs)
                    else:
                        qsel = blk - 2 * p
                        for half in range(2):
                            cs = slice(half * 256, (half + 1) * 256)
                            if half == qsel:
                                mm(so[:, cs], Kb.ap[0:64, ksl], qa.ap[0:64, cs], True, False, [KR[part][blk // 4], qa.r], st_rs)
                                mm(so[:, cs], ident.ap, trim.ap[:, kc, :], False, True, [ident.r, trim.r], st_rs)
                            else:
                                mm(so[:, cs], Kb.ap[0:96, ksl], qa.ap[0:96, cs], True, True, [KR[part][blk // 4], qa.r], st_rs)

            emit_S(0)
            if total > 1:
                emit_S(1)
            for pi in range(total):
                Kb, Vb, part, blk = pairs[pi]
                st_ap, st_rs = stiles[pi % 2]
                pt = PT2[pi % 2]
                act(pt.ap, st_ap, AF.Exp, st_rs, [pt.r])
                if pi + 2 < total:
                    emit_S(pi + 2)
                for kc in range(2):
                    cc = blk * 2 + kc
                    mm(o.ap, Vb.ap[:, cc, :], pt.ap[:, kc * 512:(kc + 1) * 512], pi == 0 and kc == 0, pi == total - 1 and kc == 1,
                       [VR[part][blk // 4], pt.r], [o.r])
            if h + 1 < NH:
                sel_finish(h + 1)
            normalise_into(o, hh, 6 + pr)
        if p + 1 < ng2:
            sel_issue(0, p + 1, t0 + 512)
            mem_attention(1, None)
            out_proj_ln_ffn(1, True, p, hook=lambda: sel_finish(0, p + 1, t0 + 512))
        else:
            mem_attention(1, None)
            out_proj_ln_ffn(1, True, p)
        dma("sp", out_d[t0:t0 + 512, :].rearrange("(s p) d -> p s d", p=128), xb.ap, xbR, [], "xbst")

    n_inst = P.finalize_and_emit(nc)
    return nc, n_inst


_CACHE = {}


def _consts(r):
    bf = ml_dtypes.bfloat16
    ident = np.eye(128, dtype=np.float32).astype(bf)
    tril = np.tril(np.ones((128, 128), np.float32))
    kpos = np.arange(128)[:, None, None] + 128 * np.arange(2)[None, :, None]
    qpos = np.arange(256)[None, None, :]
    trimask = np.where(kpos > qpos, -NEGV, 0.0).astype(np.float32).astype(bf)
    pn = np.full((16, 32), -3.0e30, np.float32)
    for i in range(16):
        j = 2 * i + r
        for n in range(32):
            jj = 2 * n + r if n < 16 else 2 * (n - 16) + (1 - r)
            if jj < j:
                pn[i, n] = 0.0
    pastneg = np.ascontiguousarray(np.broadcast_to(pn[None], (128, 16, 32)))
    ohA = np.zeros((32, 4096), np.float32)
    ohB = np.zeros((32, 4096), np.float32)
    for n in range(16):
        ohA[n, n * 256:(n + 1) * 256] = NEGV
        ohB[16 + n, n * 256:(n + 1) * 256] = NEGV
    return dict(ident=ident, tril=tril, trimask=trimask, pastneg=pastneg, ohA=ohA.astype(bf), ohB=ohB.astype(bf),
                negh=np.full((128, 1), -0.5, np.float32))


def _in_maps(inp):
    f = lambda a: np.ascontiguousarray(np.asarray(a, dtype=np.float32))
    x = f(inp["x"])
    mem = f(inp["mem"])
    lnrep = np.stack([np.stack([np.broadcast_to(f(inp[k])[L][None, :], (128, D)) for k in ("ln_mix_g", "ln_mix_b", "ln_ffn_g", "ln_ffn_b")], 1)
                      for L in range(2)], 0)
    lnrep = np.ascontiguousarray(lnrep)
    lncol = np.ascontiguousarray(np.stack([np.stack([f(inp[k])[L].reshape(8, 128).T for k in ("ln_mix_g", "ln_mix_b", "ln_ffn_g", "ln_ffn_b")], 1)
                                           for L in range(2)], 0))
    lnv = np.ascontiguousarray(np.stack([np.broadcast_to(f(inp[k])[0][None, :], (128, WMIX)) for k in ("a_ln_v_g", "a_ln_v_b")], 1))
    bs = f(inp["a_b_s"])[0]
    bsT = np.ascontiguousarray(np.repeat(bs.reshape(6, 2, 128).transpose(1, 0, 2), 64, axis=0))
    shared = dict(a_w_in=f(inp["a_w_in"])[0], a_w_s=f(inp["a_w_s"])[0], b_w_in=f(inp["b_w_in"])[0], w_mem_kv=f(inp["w_mem_kv"]),
                  w_mix_out=f(inp["w_mix_out"]), w_ffn_in=f(inp["w_ffn_in"]), w_ffn_out=f(inp["w_ffn_out"]),
                  lnrep=lnrep, lnv=lnv, bsT=bsT, lncol=lncol)
    maps = []
    for c in range(8):
        b, r = c // 2, c % 2
        xb_ = x[b].reshape(16, 2, 256, D)
        xl = np.ascontiguousarray(np.concatenate([xb_[:, r], xb_[:, 1 - r]], 0).reshape(SEQ, D))
        m = dict(shared)
        m.update(_consts(r))
        m["x"] = xl
        m["mem"] = np.ascontiguousarray(mem[b])
        maps.append(m)
    return maps


def kernel(**inputs):
    debug = bool(os.environ.get("MK_DEBUG"))
    key = ("nc", debug)
    if key not in _CACHE:
        _CACHE[key] = build_program(debug, int(os.environ.get("MK_NG1", NG1)), int(os.environ.get("MK_NG2", NG2)), int(os.environ.get("MK_CUT", 99)))
    nc, n_inst = _CACHE[key]
    maps = _in_maps(inputs)
    res = run_bass_kernel_spmd(nc, maps, core_ids=list(range(8)))
    out = np.empty((4, SEQ, D), np.float32)
    for c in range(8):
        b, r = c // 2, c % 2
        o = np.asarray(res.results[c]["out"], dtype=np.float32).reshape(16, 256, D)
        out[b].reshape(16, 2, 256, D)[:, r] = o
    if debug:
        kernel.debug = [np.asarray(res.results[c]["x1dbg"]) for c in range(8)]
    return out
```
